# Optimizing a Trainium2 kernel written in Bass

```python
import jax
import jax.numpy as jnp
from jax import lax
import numpy as np

D_MODEL = 1024
BATCH = 8
SEQ = 4096
DEPTH = 2

CTX_LEN = 256
GRID_W = 64
HEAD_DIM = 64
N_HEADS = D_MODEL // 128
N_KV_HEADS = 2
ATTN_WIDTH = N_HEADS * HEAD_DIM
KV_WIDTH = N_KV_HEADS * HEAD_DIM
Q_BLOCK = 128
ROPE_THETA = 10000.0
LRU_WIDTH = D_MODEL // 4
LRU_BLOCKS = 4
LRU_BLOCK = LRU_WIDTH // LRU_BLOCKS
LRU_CONV = 4
LRU_CONV_LEFT = 2
LRU_C = 8.0
SC_WIDTH = D_MODEL // 4
SC_CONV = 3
MIX_WIDTH = ATTN_WIDTH + LRU_WIDTH + SC_WIDTH
IN_SIZES = (ATTN_WIDTH, KV_WIDTH, KV_WIDTH,
            LRU_WIDTH, LRU_WIDTH,
            SC_WIDTH, SC_WIDTH, SC_WIDTH)
IN_SPLITS = tuple(sum(IN_SIZES[:i + 1]) for i in range(len(IN_SIZES) - 1))
IN_WIDTH = sum(IN_SIZES)
D_FF = 2816
FFN_CONV = 3
EPS = 1e-6

kernel_name = "hymba_style_rglru_gqa_shortconv_convffn_prefix_ctx"

F32 = jnp.float32


def _rmsnorm(x, g):
    xf = x.astype(F32)
    y = xf * lax.rsqrt(jnp.mean(xf * xf, axis=-1, keepdims=True) + EPS)
    return (y * g.astype(F32)).astype(x.dtype)


def _heads(t, n):
    return t.reshape(*t.shape[:-1], n, HEAD_DIM)


def _rope_tables(n_tok):
    rows = n_tok // GRID_W
    pos_r = jnp.repeat(jnp.arange(rows, dtype=F32), GRID_W)
    pos_c = jnp.tile(jnp.arange(GRID_W, dtype=F32), rows)
    n_f = HEAD_DIM // 4
    inv = ROPE_THETA ** (-jnp.arange(n_f, dtype=F32) / n_f)
    ang = jnp.concatenate([pos_r[:, None] * inv, pos_c[:, None] * inv], axis=-1)
    return jnp.cos(ang), jnp.sin(ang)


def _rope(x, cos, sin):
    xf = x.astype(F32)
    x1, x2 = jnp.split(xf, 2, axis=-1)
    cs = cos[:, None, :]
    sn = sin[:, None, :]
    return jnp.concatenate([x1 * cs - x2 * sn, x1 * sn + x2 * cs], axis=-1).astype(x.dtype)


def _attend(q, k, v):
    b, t = q.shape[0], q.shape[1]
    grp = N_HEADS // N_KV_HEADS
    nb = t // Q_BLOCK
    qb = q.reshape(b, nb, Q_BLOCK, N_KV_HEADS, grp, HEAD_DIM).swapaxes(0, 1)
    scale = HEAD_DIM ** -0.5

    def one_block(qblk):
        s = jnp.einsum("bqkgd,bskd->bkgqs", qblk, k).astype(F32) * scale
        pr = jax.nn.softmax(s, axis=-1).astype(v.dtype)
        return jnp.einsum("bkgqs,bskd->bqkgd", pr, v)

    o = lax.map(one_block, qb)
    return o.swapaxes(0, 1).reshape(b, t, N_HEADS * HEAD_DIM)


def _dwconv(x, w, left):
    kw = w.shape[0]
    t = x.shape[1]
    xp = jnp.pad(x, ((0, 0), (left, kw - 1 - left), (0, 0)))
    y = xp[:, 0:t] * w[0]
    for j in range(1, kw):
        y = y + xp[:, j:j + t] * w[j]
    return y


def _block_diag(xf, w, b):
    xb = xf.reshape(*xf.shape[:-1], LRU_BLOCKS, LRU_BLOCK)
    y = jnp.einsum("btnd,nde->btne", xb, w.astype(F32)).reshape(xf.shape)
    return y + b.astype(F32)


def _rglru_coeffs(xc, w_a, b_a, w_i, b_i, lam):
    xf = xc.astype(F32)
    r = jax.nn.sigmoid(_block_diag(xf, w_a, b_a))
    i = jax.nn.sigmoid(_block_diag(xf, w_i, b_i))
    log_a = -LRU_C * r * jax.nn.softplus(-lam.astype(F32))
    a = jnp.exp(log_a)
    bb = jnp.sqrt(-jnp.expm1(2.0 * log_a)) * (i * xf)
    return a, bb


def _linear_scan(a, b, h0, reverse):
    if h0 is not None:
        edge = -1 if reverse else 0
        b = b.at[:, edge].add(a[:, edge] * h0)

    def combine(prev, nxt):
        a_p, b_p = prev
        a_n, b_n = nxt
        return a_p * a_n, a_n * b_p + b_n

    return lax.associative_scan(combine, (a, b), axis=1, reverse=reverse)[1]


def _conv_ffn(h, p):
    up = _dwconv(h @ p["w_up"], p["ffn_conv_w"], FFN_CONV // 2)
    u, g = jnp.split(up, 2, axis=-1)
    return (jax.nn.silu(g) * u) @ p["w_down"]


def _layer(x, ctx, c, c_ctx, p, cos, sin, update_ctx):
    dt = x.dtype
    mod_x = (jax.nn.silu(c) @ p["w_mod"] + p["b_mod"])[:, None, :]
    mod_c = jax.nn.silu(c_ctx) @ p["w_mod"] + p["b_mod"]
    sa_x, ca_x, ga_x, sf_x, cf_x, gf_x = jnp.split(mod_x, 6, axis=-1)
    sa_c, ca_c, ga_c, sf_c, cf_c, gf_c = jnp.split(mod_c, 6, axis=-1)

    hx = _rmsnorm(x, p["g_mix"]) * (1 + ca_x) + sa_x
    hc = _rmsnorm(ctx, p["g_mix"]) * (1 + ca_c) + sa_c
    qx, kx, vx, rx, gx, bx, cx, ux = jnp.split(hx @ p["w_in"], IN_SPLITS, axis=-1)
    qc, kc, vc, rc, gc, bc, cc, uc = jnp.split(hc @ p["w_in"], IN_SPLITS, axis=-1)

    kc_h = _rmsnorm(_heads(kc, N_KV_HEADS), p["g_k"])
    vc_h = _heads(vc, N_KV_HEADS)
    qx_h = _rope(_rmsnorm(_heads(qx, N_HEADS), p["g_q"]), cos, sin)
    kx_h = _rope(_rmsnorm(_heads(kx, N_KV_HEADS), p["g_k"]), cos, sin)
    k_all = jnp.concatenate([kx_h, kc_h], axis=1)
    v_all = jnp.concatenate([_heads(vx, N_KV_HEADS), vc_h], axis=1)
    att_x = _attend(qx_h, k_all, v_all)

    rc_conv = _dwconv(rc, p["lru_conv_w"], LRU_CONV_LEFT) + p["lru_conv_b"]
    rx_conv = _dwconv(rx, p["lru_conv_w"], LRU_CONV_LEFT) + p["lru_conv_b"]
    dir_par = [(p["lru_wa"][d], p["lru_ba"][d], p["lru_wi"][d], p["lru_bi"][d], p["lru_lam"][d]) for d in range(2)]
    a_cf, b_cf = _rglru_coeffs(rc_conv, *dir_par[0])
    a_cb, b_cb = _rglru_coeffs(rc_conv, *dir_par[1])
    a_xf, b_xf = _rglru_coeffs(rx_conv, *dir_par[0])
    a_xb, b_xb = _rglru_coeffs(rx_conv, *dir_par[1])
    h_cf = _linear_scan(a_cf, b_cf, None, False)
    h_cb = _linear_scan(a_cb, b_cb, None, True)
    h_xf = _linear_scan(a_xf, b_xf, h_cf[:, -1], False)
    h_xb = _linear_scan(a_xb, b_xb, h_cb[:, 0], True)
    lru_x = jax.nn.gelu(gx) * (h_xf + h_xb).astype(dt)

    sc_x = bx * _dwconv(cx * ux, p["sc_conv_w"], SC_CONV // 2)

    x = x + ga_x * (jnp.concatenate([att_x, lru_x, sc_x], axis=-1) @ p["w_out"])
    x = x + gf_x * _conv_ffn(_rmsnorm(x, p["g_ffn"]) * (1 + cf_x) + sf_x, p)

    if update_ctx:
        qc_h = _rmsnorm(_heads(qc, N_HEADS), p["g_q"])
        att_c = _attend(qc_h, kc_h, vc_h)
        lru_c = jax.nn.gelu(gc) * (h_cf + h_cb).astype(dt)
        sc_c = bc * _dwconv(cc * uc, p["sc_conv_w"], SC_CONV // 2)
        ctx = ctx + ga_c * (jnp.concatenate([att_c, lru_c, sc_c], axis=-1) @ p["w_out"])
        ctx = ctx + gf_c * _conv_ffn(_rmsnorm(ctx, p["g_ffn"]) * (1 + cf_c) + sf_c, p)
    return x, ctx


def setup_inputs(seed: int = 0) -> dict:
    key = jax.random.key(seed)
    ks = iter(jax.random.split(key, 32))
    D = D_MODEL

    def nrm(shape, scale):
        return jax.random.normal(next(ks), shape, F32) * scale

    x = nrm((BATCH, SEQ, D), 1.0)
    c = nrm((BATCH, D), 1.0)
    ctx = nrm((BATCH, CTX_LEN, D), 1.0)
    c_ctx = nrm((D,), 1.0)
    w_mod = nrm((DEPTH, D, 6 * D), 0.5 * D ** -0.5)
    b_mod = nrm((DEPTH, 6 * D), 0.02)
    g_mix = 1.0 + nrm((DEPTH, D), 0.05)
    g_ffn = 1.0 + nrm((DEPTH, D), 0.05)
    w_in = nrm((DEPTH, D, IN_WIDTH), D ** -0.5)
    g_q = 1.0 + nrm((DEPTH, HEAD_DIM), 0.05)
    g_k = 1.0 + nrm((DEPTH, HEAD_DIM), 0.05)
    lru_conv_w = nrm((DEPTH, LRU_CONV, LRU_WIDTH), LRU_CONV ** -0.5)
    lru_conv_b = nrm((DEPTH, LRU_WIDTH), 0.01)
    lru_wa = nrm((DEPTH, 2, LRU_BLOCKS, LRU_BLOCK, LRU_BLOCK), LRU_BLOCK ** -0.5)
    lru_ba = nrm((DEPTH, 2, LRU_WIDTH), 0.01)
    lru_wi = nrm((DEPTH, 2, LRU_BLOCKS, LRU_BLOCK, LRU_BLOCK), LRU_BLOCK ** -0.5)
    lru_bi = nrm((DEPTH, 2, LRU_WIDTH), 0.01)
    u = jax.random.uniform(next(ks), (DEPTH, 2, LRU_WIDTH), F32, 0.9, 0.999)
    a0 = u ** (1.0 / LRU_C)
    lru_lam = jnp.log(a0) - jnp.log1p(-a0)
    sc_conv_w = nrm((DEPTH, SC_CONV, SC_WIDTH), SC_CONV ** -0.5)
    w_out = nrm((DEPTH, MIX_WIDTH, D), MIX_WIDTH ** -0.5)
    w_up = nrm((DEPTH, D, 2 * D_FF), D ** -0.5)
    ffn_conv_w = nrm((DEPTH, FFN_CONV, 2 * D_FF), FFN_CONV ** -0.5)
    w_down = nrm((DEPTH, D_FF, D), D_FF ** -0.5)
    g_final = 1.0 + nrm((D,), 0.05)
    return {"x": x, "c": c, "ctx": ctx, "c_ctx": c_ctx, "w_mod": w_mod, "b_mod": b_mod,
            "g_mix": g_mix, "g_ffn": g_ffn, "w_in": w_in, "g_q": g_q, "g_k": g_k,
            "lru_conv_w": lru_conv_w, "lru_conv_b": lru_conv_b, "lru_wa": lru_wa, "lru_ba": lru_ba,
            "lru_wi": lru_wi, "lru_bi": lru_bi, "lru_lam": lru_lam, "sc_conv_w": sc_conv_w,
            "w_out": w_out, "w_up": w_up, "ffn_conv_w": ffn_conv_w, "w_down": w_down,
            "g_final": g_final}


def reference(x, c, ctx, c_ctx, w_mod, b_mod, g_mix, g_ffn, w_in, g_q, g_k,
              lru_conv_w, lru_conv_b, lru_wa, lru_ba, lru_wi, lru_bi, lru_lam, sc_conv_w,
              w_out, w_up, ffn_conv_w, w_down, g_final):
    cos, sin = _rope_tables(x.shape[1])
    for l in range(DEPTH):
        p = {"w_mod": w_mod[l], "b_mod": b_mod[l], "g_mix": g_mix[l], "g_ffn": g_ffn[l],
             "w_in": w_in[l], "g_q": g_q[l], "g_k": g_k[l],
             "lru_conv_w": lru_conv_w[l], "lru_conv_b": lru_conv_b[l],
             "lru_wa": lru_wa[l], "lru_ba": lru_ba[l], "lru_wi": lru_wi[l], "lru_bi": lru_bi[l],
             "lru_lam": lru_lam[l], "sc_conv_w": sc_conv_w[l], "w_out": w_out[l],
             "w_up": w_up[l], "ffn_conv_w": ffn_conv_w[l], "w_down": w_down[l]}
        x, ctx = _layer(x, ctx, c, c_ctx, p, cos, sin, l < DEPTH - 1)
    return _rmsnorm(x, g_final)
```

```python
import numpy as np
import concourse.bass as bass
import concourse.mybir as mybir
from concourse.bass_utils import run_bass_kernel_spmd

F32 = mybir.dt.float32
BF16 = mybir.dt.bfloat16
AF = mybir.ActivationFunctionType
ALU = mybir.AluOpType
AX = mybir.AxisListType

CTX = 256
SEQ = 4096
T = CTX + SEQ
NT = T // 128
D = 1024
KC = 8
DFF = 2816
NF = 22
WIN_E = 2176
EPS = 1e-6
PADW = 4360
SB_BASE = 8192 + 1024
SB_LIMIT = 8192 + 220900
DEPTH = 2
ROLL = 12000

ENGS = ("pe", "act", "dve", "pool", "sp")


def pcol(tok):
    return tok + 2 if tok < CTX else tok + 4


def MM(out, lhsT, rhs, start=True, stop=True):
    return lambda e: e.matmul(out, lhsT=lhsT, rhs=rhs, start=start, stop=stop)


def TR(out, in_, identity):
    return lambda e: e.transpose(out=out, in_=in_, identity=identity)


def ACT(out, in_, func, **kw):
    return lambda e: e.activation(out=out, in_=in_, func=func, **kw)


def TT(out, in0, in1, op):
    return lambda e: e.tensor_tensor(out=out, in0=in0, in1=in1, op=op)


def TS(out, in0, s1, s2, op0, op1=None):
    if op1 is None:
        return lambda e: e.tensor_scalar(out=out, in0=in0, scalar1=s1, scalar2=None, op0=op0)
    return lambda e: e.tensor_scalar(out=out, in0=in0, scalar1=s1, scalar2=s2, op0=op0, op1=op1)


def STT(out, in0, scalar, in1, op0, op1):
    return lambda e: e.scalar_tensor_tensor(out=out, in0=in0, scalar=scalar, in1=in1, op0=op0, op1=op1)


def RCP(out, in_):
    return lambda e: e.reciprocal(out=out, in_=in_)


def CP(out, in_):
    return lambda e: e.tensor_copy(out=out, in_=in_)


def MS(ap, v):
    return lambda e: e.memset(ap, v)


def RED(out, in_, axis, op):
    return lambda e: e.tensor_reduce(out=out, in_=in_, axis=axis, op=op)


def SCAN(out, d0, d1, init):
    return lambda e: e.tensor_tensor_scan(out=out, data0=d0, data1=d1, initial=init, op0=ALU.mult, op1=ALU.add)


class Buf:
    __slots__ = ("w", "r")

    def __init__(self):
        self.w = None
        self.r = []


class Rec:
    def __init__(self):
        self.q = {e: [] for e in ENGS}
        self.cnt = {e: 0 for e in ENGS}
        self.waited = {e: {} for e in ENGS}
        self.ndsem = 8
        self.dq = {"sp": [0] * self.ndsem, "pool": [0] * self.ndsem, "act": [0] * self.ndsem}
        self.dnext = {"sp": 0, "pool": 0, "act": 0}
        self.last_d = {}

    def _deps(self, reads, writes):
        deps = []
        for b in reads:
            if b.w is not None:
                deps.append(b.w)
        for b in writes:
            if b.w is not None:
                deps.append(b.w)
            deps.extend(b.r)
        return deps

    def _waits(self, eng, deps):
        out = []
        wd = self.waited[eng]
        for t in deps:
            if t[0] == "E":
                if t[1] == "pe" and eng == "pe":
                    continue
                key = ("E", t[1])
            else:
                key = ("D", t[1], t[2])
            v = t[-1]
            if wd.get(key, 0) >= v:
                continue
            wd[key] = v
            out.append(t)
        best = {}
        for t in out:
            key = t[:-1]
            if key not in best or best[key][-1] < t[-1]:
                best[key] = t
        return list(best.values())

    def _post(self, tok, reads, writes):
        for b in writes:
            b.w = tok
            b.r = []
        for b in reads:
            b.r.append(tok)

    def grp(self, eng, fns, reads=(), writes=()):
        waits = self._waits(eng, self._deps(reads, writes))
        self.cnt[eng] += 1
        tok = ("E", eng, self.cnt[eng])
        self.q[eng].append((waits, fns, tok))
        self._post(tok, reads, writes)
        return tok

    def op(self, eng, fn, reads=(), writes=()):
        return self.grp(eng, [fn], reads, writes)

    def dma(self, qe, out_ap, in_ap, reads=(), writes=()):
        deps = self._deps(reads, writes)
        i = self.dnext[qe]
        self.dnext[qe] = (i + 1) % self.ndsem
        prev = self.dq[qe][i]
        if prev > 0:
            deps.append(("D", qe, i, prev))
        waits = self._waits(qe, deps)
        val = prev + 16
        self.dq[qe][i] = val
        tok = ("D", qe, i, val)
        self.q[qe].append((waits, [lambda e, o=out_ap, a=in_ap: e.dma_start(out=o, in_=a)], tok))
        self._post(tok, reads, writes)
        self.last_d[(qe, i)] = tok
        return tok

    def barrier(self):
        toks = []
        for e in ENGS:
            if self.cnt[e] > 0:
                toks.append(("E", e, self.cnt[e]))
        toks.extend(self.last_d.values())
        for e in ENGS:
            wd = self.waited[e]
            ws = []
            for t in toks:
                if t[0] == "E":
                    key = ("E", t[1])
                else:
                    key = ("D", t[1], t[2])
                if wd.get(key, 0) >= t[-1]:
                    continue
                wd[key] = t[-1]
                ws.append(t)
            if ws:
                self.q[e].append((ws, [], None))

    def emit(self, nc):
        from contextlib import ExitStack
        with ExitStack() as es:
            esem = {}
            for e in ENGS:
                n = (self.cnt[e] + ROLL - 1) // ROLL + 1
                esem[e] = [es.enter_context(nc.semaphore(f"s_{e}_{k}")) for k in range(n)]
            dsem = {}
            for qe in ("sp", "pool", "act"):
                dsem[qe] = [es.enter_context(nc.semaphore(f"d_{qe}_{k}")) for k in range(self.ndsem)]
            block = es.enter_context(nc.Block())

            def dowait(e, t):
                if t[0] == "E":
                    c = t[2]
                    e.wait_ge(esem[t[1]][(c - 1) // ROLL], (c - 1) % ROLL + 1)
                else:
                    e.wait_ge(dsem[t[1]][t[2]], t[3])

            def replay(name, e):
                for waits, fns, tok in self.q[name]:
                    for t in waits:
                        dowait(e, t)
                    ins = None
                    for f in fns:
                        ins = f(e)
                    if tok is not None and ins is not None:
                        if tok[0] == "E":
                            c = tok[2]
                            ins.then_inc(esem[name][(c - 1) // ROLL], 1)
                        else:
                            ins.then_inc(dsem[tok[1]][tok[2]], 16)

            @block.tensor
            def _(e):
                replay("pe", e)

            @block.scalar
            def _(e):
                replay("act", e)

            @block.vector
            def _(e):
                replay("dve", e)

            @block.gpsimd
            def _(e):
                replay("pool", e)

            @block.sync
            def _(e):
                replay("sp", e)


class SBAlloc:
    def __init__(self, nc):
        self.nc = nc
        self.off = SB_BASE
        self.n = 0
        self.peak = 0

    def mark(self):
        return self.off

    def release(self, m):
        self.off = m

    def alloc(self, shape, dtype):
        esz = 4 if dtype == F32 else 2
        nbytes = int(np.prod(shape[1:])) * esz
        nbytes = (nbytes + 63) // 64 * 64
        assert self.off + nbytes <= SB_LIMIT, f"SBUF overflow {self.off + nbytes}"
        self.n += 1
        t = self.nc.alloc_sbuf_tensor_at(f"sb{self.n}", list(shape), dtype, offset=self.off)
        self.off += nbytes
        self.peak = max(self.peak, self.off)
        return t


def build_nc(debug=False):
    nc = bass.Bass("TRN2", target_bir_lowering=False, dynamic_dma_scratch_size=8192)
    R = Rec()
    SB = SBAlloc(nc)

    def din(name, shape, dt=F32):
        return nc.dram_tensor(name, list(shape), dt, kind="ExternalInput").ap()

    def dscr(name, shape, dt=F32, out=False):
        if out:
            return nc.dram_tensor(name, list(shape), dt, kind="ExternalOutput").ap()
        return nc.dram_tensor(name, list(shape), dt).ap()

    d_xin = din("xin", [T, D])
    d_cT = din("cT", [128, 16])
    d_wmod = din("w_mod", [DEPTH, D, 6 * D])
    d_bmod = din("b_mod", [DEPTH, 6 * D])
    d_gmix = din("g_mix", [DEPTH, D])
    d_gffn = din("g_ffn", [DEPTH, D])
    d_gfin = din("g_final", [1, D])
    d_win = din("w_in_e", [DEPTH, D, WIN_E])
    d_gqk = din("gqk", [DEPTH, 768])
    d_cs = din("cs", [T, 128])
    d_lcw = din("lru_cw", [DEPTH, 128, 2, 4])
    d_lcb = din("lru_cb", [DEPTH, 128, 2])
    d_wbd = din("wbd", [DEPTH, 2, 2, 2, 128, 128])
    d_lvec = din("lru_vec", [DEPTH, 128, 2, 2, 3])
    d_scw = din("sc_cw", [DEPTH, 128, 2, 3])
    d_wout = din("w_out", [DEPTH, D, D])
    d_wup = din("wup_r", [DEPTH, NF * 128, 2048])
    d_fcw = din("fcw", [DEPTH, 128, 44, 3])
    d_wdn = din("w_down", [DEPTH, DFF, D])
    d_ident = din("ident", [128, 128])
    d_y = nc.dram_tensor("y", [SEQ, D], F32, kind="ExternalOutput").ap()

    d_modd = dscr("modd", [DEPTH, 2, 6 * D], out=debug)
    d_res = dscr("res", [T, D], out=debug)
    d_rT = dscr("rT", [2, 128, T], out=debug)
    d_gT = dscr("gT", [2, 128, T])
    d_bT = dscr("bT", [2, 128, T])
    d_cuT = dscr("cuT", [2, 128, T])
    d_mixT = dscr("mixT", [8, 128, T], BF16, out=debug)
    d_wupb = dscr("wupb", [DEPTH, NF * 128, 2048], BF16)
    d_winb = dscr("winb", [DEPTH, D, WIN_E], BF16)
    d_woutb = dscr("woutb", [DEPTH, D, D], BF16)
    d_wdnb = dscr("wdnb", [DEPTH, DFF, D], BF16)

    pbig = [nc.alloc_psum_tensor(f"pb{j}", [128, 1024], F32) for j in range(4)]

    def bank(i):
        return pbig[i // 2][:, (i % 2) * 512:(i % 2) * 512 + 512]

    def bank_bf(i):
        return bank(i).bitcast(BF16)

    pbufs = [Buf() for _ in range(8)]

    ident = SB.alloc([128, 128], BF16)
    ones_f = SB.alloc([128, 128], F32)
    neghalf = SB.alloc([128, 16], F32)
    b_const = Buf()
    R.dma("pool", ident[:], d_ident[:, :], writes=[b_const])
    R.op("pool", MS(ones_f[:], 1.0), writes=[b_const])
    R.op("pool", MS(neghalf[:], -0.5), writes=[b_const])
    b_wupb = Buf()
    b_winb = Buf()
    b_w3b = [Buf() for _ in range(DEPTH)]
    base_mark = SB.mark()

    def phase0():
        cT = SB.alloc([128, 16], F32)
        scT = SB.alloc([128, 16], BF16)
        bm = SB.alloc([2, 6 * D], F32)
        m2 = SB.alloc([2, 6 * D], F32)
        NW = 6
        wm = [SB.alloc([128, 3072], BF16) for _ in range(NW)]
        b_wm = [Buf() for _ in range(NW)]
        b_cT, b_scT, b_bm, b_m2 = Buf(), Buf(), Buf(), Buf()
        R.dma("sp", cT[:], d_cT[:, :], writes=[b_cT])
        R.op("act", ACT(scT[:], cT[:], AF.Silu), reads=[b_cT], writes=[b_scT])
        k = 0
        for l in range(DEPTH):
            R.dma("sp", bm[:], d_bmod[l:l + 1, :].partition_broadcast(2), writes=[b_bm])
            for half in range(2):
                for kc in range(KC):
                    s = k % NW
                    k += 1
                    R.dma("pool", wm[s][:], d_wmod[l, kc * 128:(kc + 1) * 128, half * 3072:(half + 1) * 3072],
                          writes=[b_wm[s]])
                    fns = [MM(bank(cb)[0:2, :], scT[:, 2 * kc:2 * kc + 2], wm[s][:, cb * 512:(cb + 1) * 512],
                              start=(kc == 0), stop=(kc == KC - 1)) for cb in range(6)]
                    R.grp("pe", fns, reads=[b_scT, b_wm[s]], writes=pbufs[0:6])
                for cb in range(6):
                    c0 = half * 3072 + cb * 512
                    R.op("dve", TT(m2[0:2, c0:c0 + 512], bank(cb)[0:2, :], bm[0:2, c0:c0 + 512], ALU.add),
                         reads=[pbufs[cb], b_bm], writes=[b_m2])
            for c0 in (1024, 4096):
                R.op("dve", TS(m2[0:2, c0:c0 + 1024], m2[0:2, c0:c0 + 1024], 1.0, None, ALU.add), reads=[b_m2], writes=[b_m2])
            R.dma("sp", d_modd[l, :, :], m2[:], reads=[b_m2])

    phase0()
    R.barrier()
    SB.release(base_mark)

    bgq = []

    def emit_bg_casts():
        def add(dst, src, buf):
            bgq.append(lambda: R.dma("pool", dst, src, writes=[buf]))
        def pieces(dst, src, nrows, npc, buf):
            step = nrows // npc
            for part in range(npc):
                add(dst[part * step:(part + 1) * step, :], src[part * step:(part + 1) * step, :], buf)
        for l in range(DEPTH):
            if l > 0:
                pieces(d_winb[l], d_win[l], D, 4, b_winb)
            pieces(d_woutb[l], d_wout[l], D, 4, b_w3b[l])
            pieces(d_wdnb[l], d_wdn[l], DFF, 8, b_w3b[l])
            pieces(d_wupb[l], d_wup[l], NF * 128, 8, b_wupb)

    def bg_tick():
        if bgq:
            bgq.pop(0)()

    def load_rep(tile, l, row, seg, buf):
        R.dma("sp", tile[:], d_modd[l, row:row + 1, seg * D:(seg + 1) * D].partition_broadcast(128), writes=[buf])

    def v3(ap, d):
        return ap.rearrange("p (h d) -> p h d", d=d)

    def layer(l):
        last = (l == DEPTH - 1)
        seq_in = d_xin if l == 0 else d_res
        lmark = SB.mark()
        qT = SB.alloc([128, 4, T], BF16)
        kT = SB.alloc([128, 4, T], BF16)
        VB = SB.alloc([128, NT, 2, 129], BF16)
        b_qT = [Buf() for _ in range(NT)]
        b_VB = Buf()

        p1mark = SB.mark()
        win = SB.alloc([128, KC, WIN_E], BF16)
        b_win = Buf()
        for kc in range(KC):
            if l == 0:
                R.dma("pool", win[:, kc, :], d_win[l, kc * 128:(kc + 1) * 128, :], writes=[b_win])
            else:
                R.dma("sp", win[:, kc, :], d_winb[l, kc * 128:(kc + 1) * 128, :], reads=[b_winb], writes=[b_win])
        if l == 0:
            emit_bg_casts()
        GX, SX = (SB.alloc([128, D], F32) for _ in range(2))
        b_rep = Buf()

        def load_mix_reps(row):
            load_rep(GX, l, row, 1, b_rep)
            load_rep(SX, l, row, 0, b_rep)
            R.dma("sp", TMP[0][:], d_gmix[l:l + 1, :].partition_broadcast(128), writes=[b_TMP[0]])
            R.op("dve", TT(GX[:], GX[:], TMP[0][:], ALU.mult), reads=[b_rep, b_TMP[0]], writes=[b_rep])

        gqk = SB.alloc([128, 768], F32)
        R.dma("sp", gqk[:], d_gqk[l:l + 1, :].partition_broadcast(128), writes=[b_rep])
        R.op("pool", MS(kT[:], 0.0), writes=[b_VB])
        R.op("pool", MS(VB[:], 0.0), writes=[b_VB])
        R.op("pool", MS(VB[:, :, :, 0:1], 1.0), writes=[b_VB])
        R.op("pool", MS(VB[:, :, :, 128:129], 1.0), writes=[b_VB])

        NX = 2
        XS = [SB.alloc([128, D], F32) for _ in range(NX)]
        b_XS = [Buf() for _ in range(NX)]
        NTAB = 6
        TAB = [SB.alloc([128, 128], F32) for _ in range(NTAB)]
        b_TAB = [Buf() for _ in range(NTAB)]
        TMP = [SB.alloc([128, D], F32) for _ in range(2)]
        b_TMP = [Buf() for _ in range(2)]
        NH = 4
        Hb = [SB.alloc([128, D], BF16) for _ in range(NH)]
        b_H = [Buf() for _ in range(NH)]
        hT = [SB.alloc([128, KC, 512], BF16) for _ in range(2)]
        b_hT = [[Buf() for _ in range(4)] for _ in range(2)]
        st1 = [SB.alloc([128, 16], F32) for _ in range(2)]
        b_st1 = [Buf() for _ in range(2)]
        NQ = 3
        QK = [SB.alloc([128, 768], F32) for _ in range(NQ)]
        SQ = [SB.alloc([128, 768], F32) for _ in range(NQ)]
        QG = [SB.alloc([128, 768], F32) for _ in range(NQ)]
        QO = [SB.alloc([128, 768], BF16) for _ in range(NQ)]
        st2 = [SB.alloc([128, 48], F32) for _ in range(NQ)]
        b_QK, b_SQ, b_QG, b_QO, b_st2 = ([Buf() for _ in range(NQ)] for _ in range(5))
        NSTG = 2
        STG = [SB.alloc([128, 512], F32) for _ in range(NSTG)]
        b_STG = [Buf() for _ in range(NSTG)]
        STC = [SB.alloc([128, 512], F32) for _ in range(2)]
        b_STC = [Buf() for _ in range(2)]

        blocks = [(0, 256)] + [(CTX + 512 * j, 512) for j in range(8)]
        tinfo = []
        for bi, (t0, nb) in enumerate(blocks):
            for i in range(nb // 128):
                tinfo.append((bi, i, t0 // 128 + i, t0, nb, nb // 128))
        stg = {"i": 0}

        def stA(k):
            bi, i, g, t0, nb, ntile = tinfo[k]
            if k == 0:
                load_mix_reps(1)
            elif k == 2:
                load_mix_reps(0)
            xs = k % NX
            ts = k % 2
            tb = k % NTAB
            hb = k % NH
            R.dma("sp", XS[xs][:], seq_in[g * 128:(g + 1) * 128, :], writes=[b_XS[xs]])
            R.dma("sp", TAB[tb][:], d_cs[g * 128:(g + 1) * 128, :], writes=[b_TAB[tb]])
            R.op("act", ACT(TMP[ts][:], XS[xs][:], AF.Square, accum_out=st1[ts][:, 0:1]),
                 reads=[b_XS[xs]], writes=[b_TMP[ts], b_st1[ts]])
            R.op("dve", TS(st1[ts][:, 1:2], st1[ts][:, 0:1], 1.0 / D, EPS, ALU.mult, ALU.add),
                 reads=[b_st1[ts]], writes=[b_st1[ts]])
            R.op("pool", TT(st1[ts][:, 2:3], st1[ts][:, 1:2], neghalf[:, 0:1], ALU.pow),
                 reads=[b_st1[ts]], writes=[b_st1[ts]])
            R.op("dve", STT(TMP[ts][:], XS[xs][:], st1[ts][:, 2:3], GX[:], ALU.mult, ALU.mult),
                 reads=[b_XS[xs], b_st1[ts], b_rep], writes=[b_TMP[ts]])
            R.op("dve", TT(Hb[hb][:], TMP[ts][:], SX[:], ALU.add), reads=[b_TMP[ts], b_rep], writes=[b_H[hb]])

        def stP1(k):
            bi, i, g, t0, nb, ntile = tinfo[k]
            hs = bi % 2
            hb = k % NH
            qs = k % NQ
            tb = k % NTAB
            ps = k % 2
            tcols = slice(i * 128, (i + 1) * 128)
            fns = [TR(bank_bf(0)[:, kc * 128:(kc + 1) * 128], Hb[hb][:, kc * 128:(kc + 1) * 128], ident[:]) for kc in range(KC)]
            R.grp("pe", fns, reads=[b_H[hb], b_const], writes=[pbufs[0]])
            R.op("act", ACT(hT[hs][:, :, tcols], v3(bank_bf(0), 128), AF.Copy), reads=[pbufs[0]], writes=[b_hT[hs][i]])

        def stP2(k):
            bi, i, g, t0, nb, ntile = tinfo[k]
            hs = bi % 2
            qs = k % NQ
            tb = k % NTAB
            ps = k % 2
            tcols = slice(i * 128, (i + 1) * 128)
            pq = 1 + 2 * ps
            pkv = 2 + 2 * ps
            fns = [MM(bank(pq), hT[hs][:, kc, tcols], win[:, kc, 0:512], start=(kc == 0), stop=(kc == KC - 1)) for kc in range(KC)]
            fns += [MM(bank(pkv)[:, 0:384], hT[hs][:, kc, tcols], win[:, kc, 512:896], start=(kc == 0), stop=(kc == KC - 1))
                    for kc in range(KC)]
            R.grp("pe", fns, reads=[b_hT[hs][i], b_win], writes=[pbufs[pq], pbufs[pkv]])
            R.op("act", ACT(QK[qs][:, 0:512], bank(pq), AF.Copy), reads=[pbufs[pq]], writes=[b_QK[qs]])
            R.op("act", ACT(QK[qs][:, 512:768], bank(pkv)[:, 0:256], AF.Copy), reads=[pbufs[pkv]], writes=[b_QK[qs]])
            R.op("act", ACT(VB[:, g, :, 64:128], v3(bank(pkv)[:, 256:384], 64), AF.Copy), reads=[pbufs[pkv]], writes=[b_VB])
            R.op("dve", TT(QG[qs][:], QK[qs][:], gqk[:], ALU.mult), reads=[b_QK[qs], b_rep], writes=[b_QG[qs]])
            R.op("act", ACT(SQ[qs][:], QK[qs][:], AF.Square), reads=[b_QK[qs]], writes=[b_SQ[qs]])
            R.op("dve", RED(st2[qs][:, 0:12], v3(SQ[qs][:], 64), AX.X, ALU.add), reads=[b_SQ[qs]], writes=[b_st2[qs]])
            R.op("dve", TS(st2[qs][:, 12:24], st2[qs][:, 0:12], 1.0 / 64, EPS, ALU.mult, ALU.add),
                 reads=[b_st2[qs]], writes=[b_st2[qs]])
            R.op("pool", TT(st2[qs][:, 24:36], st2[qs][:, 12:24], neghalf[:, 0:12], ALU.pow),
                 reads=[b_st2[qs]], writes=[b_st2[qs]])
            R.op("pool", TT(v3(QK[qs][:], 64), v3(QG[qs][:], 64),
                            TAB[tb][:, 0:64].unsqueeze(1).to_broadcast([128, 12, 64]), ALU.mult),
                 reads=[b_QG[qs], b_TAB[tb], b_QK[qs]], writes=[b_QK[qs]])
            SQv = SQ[qs][:].rearrange("p (h t d) -> p h t d", t=2, d=32)
            QGv = QG[qs][:].rearrange("p (h t d) -> p h t d", t=2, d=32)
            R.op("dve", TT(SQv[:, :, 0, :], QGv[:, :, 1, :], TAB[tb][:, 64:96].unsqueeze(1).to_broadcast([128, 12, 32]), ALU.mult),
                 reads=[b_QG[qs], b_TAB[tb], b_SQ[qs], b_st2[qs]], writes=[b_SQ[qs]])
            R.op("dve", TT(SQv[:, :, 1, :], QGv[:, :, 0, :], TAB[tb][:, 96:128].unsqueeze(1).to_broadcast([128, 12, 32]), ALU.mult),
                 reads=[b_QG[qs], b_TAB[tb], b_SQ[qs], b_st2[qs]], writes=[b_SQ[qs]])
            R.op("pool", TT(QK[qs][:], QK[qs][:], SQ[qs][:], ALU.add), reads=[b_QK[qs], b_SQ[qs]], writes=[b_QK[qs]])
            R.op("pool", TT(v3(QO[qs][:], 64), v3(QK[qs][:], 64),
                            st2[qs][:, 24:36].unsqueeze(2).to_broadcast([128, 12, 64]), ALU.mult),
                 reads=[b_QK[qs], b_st2[qs]], writes=[b_QO[qs]])
            if i == ntile - 1:
                stFM(bi)

        def stFM(bi):
            t0, nb = blocks[bi]
            hs = bi % 2
            ntile = nb // 128
            for fc in range(10):
                pf = 6 + (fc % 2)
                fns = [MM(bank(pf)[:, 0:nb], win[:, kc, 896 + fc * 128:896 + (fc + 1) * 128], hT[hs][:, kc, 0:nb],
                          start=(kc == 0), stop=(kc == KC - 1)) for kc in range(KC)]
                R.grp("pe", fns, reads=b_hT[hs][0:ntile] + [b_win], writes=[pbufs[pf]])
                ch = fc % 2
                if fc < 6:
                    s_ = stg["i"] % NSTG
                    stg["i"] += 1
                    dst = (d_rT, d_gT, d_bT)[fc // 2]
                    R.op("act", ACT(STG[s_][:, 0:nb], bank(pf)[:, 0:nb], AF.Copy), reads=[pbufs[pf]], writes=[b_STG[s_]])
                    R.dma("act", dst[ch, :, t0:t0 + nb], STG[s_][:, 0:nb], reads=[b_STG[s_]])
                elif fc < 8:
                    R.op("act", ACT(STC[ch][:, 0:nb], bank(pf)[:, 0:nb], AF.Copy), reads=[pbufs[pf]], writes=[b_STC[ch]])
                else:
                    s_ = stg["i"] % NSTG
                    stg["i"] += 1
                    R.op("dve", TT(STG[s_][:, 0:nb], bank(pf)[:, 0:nb], STC[ch][:, 0:nb], ALU.mult),
                         reads=[pbufs[pf], b_STC[ch]], writes=[b_STG[s_]])
                    R.dma("pool", d_cuT[ch, :, t0:t0 + nb], STG[s_][:, 0:nb], reads=[b_STG[s_]])

        def stP3(k):
            bi, i, g, t0, nb, ntile = tinfo[k]
            qs = k % NQ
            gcols = slice(g * 128, (g + 1) * 128)
            fns = [TR(bank_bf(5)[:, j * 128:(j + 1) * 128], QO[qs][:, j * 128:(j + 1) * 128], ident[:]) for j in range(6)]
            R.grp("pe", fns, reads=[b_QO[qs], b_const], writes=[pbufs[5]])
            R.op("act", ACT(qT[:, :, gcols], v3(bank_bf(5)[:, 0:512], 128), AF.Copy), reads=[pbufs[5]], writes=[b_qT[g]])
            R.op("act", ACT(kT[0:64, 0::2, gcols], v3(bank_bf(5)[0:64, 512:768], 128), AF.Copy),
                 reads=[pbufs[5], b_VB], writes=[b_qT[g]])
            R.op("act", ACT(kT[64:128, 1::2, gcols], v3(bank_bf(5)[64:128, 512:768], 128), AF.Copy),
                 reads=[pbufs[5], b_VB], writes=[b_qT[g]])

        LA, L2, L3 = 3, 1, 2
        for k in range(NT + LA + L2 + L3):
            if k < NT:
                stA(k)
            if k % 2 == 1:
                bg_tick()
            if 0 <= k - LA < NT:
                stP1(k - LA)
            if 0 <= k - LA - L2 < NT:
                stP2(k - LA - L2)
            if 0 <= k - LA - L2 - L3 < NT:
                stP3(k - LA - L2 - L3)
        R.barrier()
        SB.release(p1mark)

        Pb = [SB.alloc([128, 1024], BF16) for _ in range(3)]
        b_P = [Buf() for _ in range(3)]
        REC = [SB.alloc([128, 512], F32) for _ in range(1)] * 2
        E0f = SB.alloc([128, 128], F32)
        E64f = SB.alloc([128, 128], F32)
        b_sel = Buf()
        R.op("pool", MS(E0f[:], 0.0), writes=[b_sel])
        R.op("pool", MS(E64f[:], 0.0), writes=[b_sel])
        R.op("pool", MS(E0f[0:1, :], 1.0), writes=[b_sel])
        R.op("pool", MS(E64f[64:65, :], 1.0), writes=[b_sel])
        BCS = [SB.alloc([128, 512], F32) for _ in range(2)]
        ATT = [SB.alloc([128, 512], BF16) for _ in range(2)]
        b_BCS, b_ATT = ([Buf() for _ in range(2)] for _ in range(2))
        b_REC = [Buf()] * 2
        R.op("pool", MS(REC[0][:], 0.0), writes=[b_REC[0]])
        lcw = SB.alloc([128, 8], F32)
        lcb = SB.alloc([128, 2], F32)
        lvec = SB.alloc([128, 12], F32)
        lder = SB.alloc([128, 48], F32)
        scw = SB.alloc([128, 6], F32)
        wbd = SB.alloc([128, 8, 128], BF16)
        RRG = SB.alloc([128, PADW], F32)
        RC = SB.alloc([128, PADW], F32)
        IG = SB.alloc([128, PADW], F32)
        T1 = SB.alloc([128, PADW], F32)
        HF = SB.alloc([128, PADW], F32)
        HBk = SB.alloc([128, PADW], F32)
        RCb = SB.alloc([128, PADW], BF16)
        NLB = (PADW + 511) // 512
        b_lc = Buf()
        b_RRG, b_RC, b_IG, b_T1, b_HF, b_HB, b_RCb = ([Buf() for _ in range(NLB)] for _ in range(7))
        lv3 = lvec[:].rearrange("p (i k) -> p i k", k=3)
        xcols = slice(4 + CTX, 4 + T)
        ccols = slice(2, 2 + CTX)
        lblocks = [(j * 512, min(512, PADW - j * 512)) for j in range(NLB)]
        lphases = []

        def l_setup():
            R.dma("sp", lcw[:], d_lcw[l].rearrange("p c k -> p (c k)"), writes=[b_lc])
            R.dma("sp", lcb[:], d_lcb[l], writes=[b_lc])
            R.dma("sp", lvec[:], d_lvec[l].rearrange("p a c k -> p (a c k)"), writes=[b_lc])
            R.dma("sp", scw[:], d_scw[l].rearrange("p c k -> p (c k)"), writes=[b_lc])
            for j in range(8):
                dd, gg, cc = j // 4, (j // 2) % 2, j % 2
                R.dma("pool", wbd[:, j, :], d_wbd[l, dd, gg, cc, :, :], writes=[b_lc])

        def l_pads():
            for tns, bb_ in ((HF, b_HF), (HBk, b_HB)):
                R.op("pool", MS(tns[:, 0:2], 0.0), writes=[bb_[0]])
                R.op("pool", MS(tns[:, 258:260], 0.0), writes=[bb_[0]])
                R.op("pool", MS(tns[:, 4356:PADW], 0.0), writes=[bb_[NLB - 1]])

        def l_setup2():
            R.op("act", ACT(lder[:, 16:20], lv3[:, :, 2], AF.Exp, scale=-1.0), reads=[b_lc], writes=[b_lc])
            R.op("act", ACT(lder[:, 20:24], lder[:, 16:20], AF.Ln, bias=1.0), reads=[b_lc], writes=[b_lc])

        def l_setup3():
            R.op("dve", TS(lder[:, 0:4], lv3[:, :, 0], -1.0, None, ALU.mult), reads=[b_lc], writes=[b_lc])
            R.op("dve", TS(lder[:, 4:8], lv3[:, :, 1], -1.0, None, ALU.mult), reads=[b_lc], writes=[b_lc])
            R.op("dve", TS(lder[:, 8:12], lder[:, 20:24], -8.0, None, ALU.mult), reads=[b_lc], writes=[b_lc])
            R.op("dve", TS(lder[:, 12:16], lder[:, 20:24], -16.0, None, ALU.mult), reads=[b_lc], writes=[b_lc])
        lphases.append((1, [[l_setup, l_setup2, l_setup3], [l_pads]]))

        def mk_load(dst, b_dst, src, c):
            def f():
                R.op("pool", MS(dst[:, 0:2], 0.0), writes=b_dst)
                R.op("pool", MS(dst[:, 258:260], 0.0), writes=b_dst)
                R.op("pool", MS(dst[:, 4356:PADW], 0.0), writes=b_dst)
                R.dma("sp", dst[:, ccols], src[c, :, 0:CTX], writes=b_dst)
                R.dma("sp", dst[:, xcols], src[c, :, CTX:T], writes=b_dst)
            return f

        def conv_block(j, c0, w, cw, nk, left, bias):
            a = max(c0, 2)
            b = min(c0 + w, 4358)
            n1 = b - a

            def f():
                for k in range(nk):
                    src = RRG[:, a + k - left:a + k - left + n1]
                    if k == 0:
                        if bias is not None:
                            R.op("dve", TS(RC[:, a:b], src, cw[:, 0:1], bias, ALU.mult, ALU.add), reads=b_RRG + [b_lc], writes=[b_RC[j]])
                        else:
                            R.op("dve", TS(RC[:, a:b], src, cw[:, 0:1], None, ALU.mult), reads=b_RRG + [b_lc], writes=[b_RC[j]])
                    else:
                        R.op("dve", STT(RC[:, a:b], src, cw[:, k:k + 1], RC[:, a:b], ALU.mult, ALU.add),
                             reads=b_RRG + [b_lc, b_RC[j]], writes=[b_RC[j]])
            return f

        for c in range(2):
            lphases.append((1, [[mk_load(RRG, b_RRG, d_rT, c)]]))
            lphases.append((1, [[lambda: None], [lambda: None], [lambda: None]]))
            steps = []
            for j, (c0, w) in enumerate(lblocks):
                cs_ = slice(c0, c0 + w)
                sub = [conv_block(j, c0, w, lcw[:, c * 4:c * 4 + 4], 4, 2, lcb[:, c:c + 1])]
                if j == 0:
                    sub.append(lambda: R.op("pool", MS(RC[:, 0:2], 0.0), writes=[b_RC[0]]))
                if j == NLB - 1:
                    sub.append(lambda: R.op("pool", MS(RC[:, 4358:PADW], 0.0), writes=[b_RC[NLB - 1]]))
                sub.append(lambda j=j, cs_=cs_: R.op("dve", CP(RCb[:, cs_], RC[:, cs_]), reads=[b_RC[j]], writes=[b_RCb[j]]))
                steps.append(sub)
            lphases.append((1, steps))
            for dd in range(2):
                idx = dd * 2 + c
                steps = []
                for j, (c0, w) in enumerate(lblocks):
                    cs_ = slice(c0, c0 + w)

                    def mma(j=j, cs_=cs_, w=w, dd=dd, c=c):
                        R.grp("pe", [MM(bank(7)[:, 0:w], wbd[:, dd * 4 + c, :], RCb[:, cs_])], reads=[b_RCb[j], b_lc], writes=[pbufs[7]])

                    def siga(j=j, cs_=cs_, w=w, idx=idx):
                        R.op("act", ACT(RRG[:, cs_], bank(7)[:, 0:w], AF.Exp, scale=-1.0, bias=lder[:, idx:idx + 1]),
                             reads=[pbufs[7], b_lc], writes=[b_RRG[j]])

                    def reca(j=j, cs_=cs_):
                        R.op("dve", TS(T1[:, cs_], RRG[:, cs_], 1.0, None, ALU.add), reads=[b_RRG[j]], writes=[b_T1[j]])
                        R.op("dve", RCP(RRG[:, cs_], T1[:, cs_]), reads=[b_T1[j]], writes=[b_RRG[j]])

                    def mmi(j=j, cs_=cs_, w=w, dd=dd, c=c):
                        R.grp("pe", [MM(bank(7)[:, 0:w], wbd[:, dd * 4 + 2 + c, :], RCb[:, cs_])], reads=[b_RCb[j], b_lc], writes=[pbufs[7]])

                    def sigi(j=j, cs_=cs_, w=w, idx=idx):
                        R.op("act", ACT(IG[:, cs_], bank(7)[:, 0:w], AF.Exp, scale=-1.0, bias=lder[:, 4 + idx:5 + idx]),
                             reads=[pbufs[7], b_lc], writes=[b_IG[j]])

                    def reci(j=j, cs_=cs_):
                        R.op("dve", TS(T1[:, cs_], IG[:, cs_], 1.0, None, ALU.add), reads=[b_IG[j]], writes=[b_T1[j]])
                        R.op("dve", RCP(IG[:, cs_], T1[:, cs_]), reads=[b_T1[j]], writes=[b_IG[j]])

                    def e4a(j=j, cs_=cs_, idx=idx):
                        R.op("act", ACT(RRG[:, cs_], RRG[:, cs_], AF.Exp, scale=lder[:, 8 + idx:9 + idx]), reads=[b_RRG[j], b_lc], writes=[b_RRG[j]])

                    def e4b(j=j, cs_=cs_):
                        R.op("pool", TT(T1[:, cs_], RRG[:, cs_], RRG[:, cs_], ALU.mult), reads=[b_RRG[j]], writes=[b_T1[j]])

                    def e4(j=j, cs_=cs_, idx=idx):
                        R.op("act", ACT(T1[:, cs_], T1[:, cs_], AF.Ln, scale=-1.0, bias=1.0), reads=[b_T1[j]], writes=[b_T1[j]])
                        R.op("act", ACT(T1[:, cs_], T1[:, cs_], AF.Exp, scale=0.5), reads=[b_T1[j]], writes=[b_T1[j]])

                    def d2(j=j, cs_=cs_):
                        R.op("pool", TT(IG[:, cs_], IG[:, cs_], RC[:, cs_], ALU.mult), reads=[b_IG[j], b_RC[j]], writes=[b_IG[j]])
                        R.op("pool", TT(T1[:, cs_], T1[:, cs_], IG[:, cs_], ALU.mult), reads=[b_T1[j], b_IG[j]], writes=[b_T1[j]])
                    steps.append([mma, siga, (lambda mmi=mmi, reca=reca: (mmi(), reca())), sigi, reci, (lambda: None), e4a, e4b, (lambda: None), e4, d2])
                lphases.append((0.25, steps))
                if dd == 0:
                    lphases.append((1, [[lambda: R.op("dve", SCAN(HF[:, ccols], RRG[:, ccols], T1[:, ccols], 0.0), reads=b_RRG + b_T1, writes=b_HF)],
                                        [lambda: R.op("dve", SCAN(HF[:, xcols], RRG[:, xcols], T1[:, xcols], HF[:, 257:258]),
                                                      reads=b_RRG + b_T1 + b_HF, writes=b_HF)]]))
                else:
                    lphases.append((1, [[lambda: R.op("dve", SCAN(HBk[:, 2:258][:, ::-1], RRG[:, 2:258][:, ::-1], T1[:, 2:258][:, ::-1], 0.0),
                                                      reads=b_RRG + b_T1, writes=b_HB)],
                                        [lambda: R.op("dve", SCAN(HBk[:, 260:4356][:, ::-1], RRG[:, 260:4356][:, ::-1], T1[:, 260:4356][:, ::-1],
                                                                  HBk[:, 2:3]), reads=b_RRG + b_T1 + b_HB, writes=b_HB)]]))
            lphases.append((1, [[mk_load(RRG, b_RRG, d_gT, c)]]))
            lphases.append((1, [[lambda: None], [lambda: None], [lambda: None]]))
            steps = []
            for j, (c0, w) in enumerate(lblocks):
                cs_ = slice(c0, c0 + w)

                def g1(j=j, cs_=cs_):
                    R.op("pool", TT(RC[:, cs_], RRG[:, cs_], RRG[:, cs_], ALU.mult), reads=[b_RRG[j]], writes=[b_RC[j]])
                    R.op("pool", TS(RC[:, cs_], RC[:, cs_], 0.044715, 1.0, ALU.mult, ALU.add), reads=[b_RC[j]], writes=[b_RC[j]])
                    R.op("pool", TT(RC[:, cs_], RC[:, cs_], RRG[:, cs_], ALU.mult), reads=[b_RC[j], b_RRG[j]], writes=[b_RC[j]])
                    R.op("pool", TT(HF[:, cs_], HF[:, cs_], HBk[:, cs_], ALU.add), reads=[b_HF[j], b_HB[j]], writes=[b_HF[j]])

                def g2(j=j, cs_=cs_):
                    R.op("act", ACT(IG[:, cs_], RC[:, cs_], AF.Exp, scale=-1.5957691216057308), reads=[b_RC[j]], writes=[b_IG[j]])

                def g3(j=j, cs_=cs_):
                    R.op("dve", TS(IG[:, cs_], IG[:, cs_], 1.0, None, ALU.add), reads=[b_IG[j]], writes=[b_IG[j]])
                    R.op("dve", RCP(T1[:, cs_], IG[:, cs_]), reads=[b_IG[j]], writes=[b_T1[j]])
                    R.op("dve", TT(T1[:, cs_], T1[:, cs_], RRG[:, cs_], ALU.mult), reads=[b_T1[j], b_RRG[j]], writes=[b_T1[j]])
                    R.op("dve", TT(RCb[:, cs_], T1[:, cs_], HF[:, cs_], ALU.mult), reads=[b_T1[j], b_HF[j]], writes=[b_RCb[j]])
                steps.append([g1, (lambda: None), (lambda: None), g2, g3])
            lphases.append((0.5, steps))

            def lout(c=c):
                R.dma("sp", d_mixT[4 + c, :, 0:CTX], RCb[:, ccols], reads=b_RCb)
                R.dma("sp", d_mixT[4 + c, :, CTX:T], RCb[:, xcols], reads=b_RCb)
            lphases.append((1, [[lout]]))
        for c in range(2):
            def loadb(c=c):
                R.dma("sp", HBk[:, ccols], d_bT[c, :, 0:CTX], writes=b_HB)
                R.dma("sp", HBk[:, xcols], d_bT[c, :, CTX:T], writes=b_HB)
            lphases.append((1, [[mk_load(RRG, b_RRG, d_cuT, c), loadb]]))
            lphases.append((1, [[lambda: None], [lambda: None], [lambda: None]]))
            steps = []
            for j, (c0, w) in enumerate(lblocks):
                a = max(c0, 2)
                b = min(c0 + w, 4356)

                def scm(j=j, a=a, b=b):
                    R.op("pool", TT(RCb[:, a:b], RC[:, a:b], HBk[:, a:b], ALU.mult), reads=[b_RC[j], b_HB[j]], writes=[b_RCb[j]])
                steps.append([conv_block(j, c0, w, scw[:, c * 3:c * 3 + 3], 3, 1, None), scm])
            lphases.append((1, steps))

            def sout(c=c):
                R.dma("sp", d_mixT[6 + c, :, 0:CTX], RCb[:, ccols], reads=b_RCb)
                R.dma("sp", d_mixT[6 + c, :, CTX:T], RCb[:, xcols], reads=b_RCb)
            lphases.append((1, [[sout]]))

        lstate = {"ph": 0, "next": 0, "active": []}

        def l_slot():
            act = lstate["active"]
            for stp in list(act):
                stp.pop(0)()
                if not stp:
                    act.remove(stp)
            while lstate["ph"] < len(lphases):
                rate, steps = lphases[lstate["ph"]]
                if lstate["next"] < len(steps):
                    lstate["tick"] = lstate.get("tick", 0) + 1
                    if rate >= 1:
                        nnew = int(rate)
                    else:
                        per = int(round(1 / rate))
                        nnew = 1 if lstate["tick"] % per == 1 else 0
                    for _ in range(nnew):
                        if lstate["next"] < len(steps):
                            act.append(list(steps[lstate["next"]]))
                            lstate["next"] += 1
                    return True
                if act:
                    return True
                lstate["ph"] += 1
                lstate["next"] = 0
            return bool(act)

        qblocks = []
        if not last:
            qblocks.append((0, 256, [0, 1]))
        for j in range(8):
            qblocks.append((CTX + 512 * j, 512, list(range(NT))))
        units = []
        ai = 0
        for (q0, nq, keys) in qblocks:
            for c in range(4):
                asl = ai % 2
                ai += 1
                for half in range(2):
                    h = 2 * c + half
                    u = dict(q0=q0, nq=nq, keys=keys, c=c, half=half, kv=h // 4, p0=half * 64, asl=asl,
                             npair=len(keys) // 2, idx=len(units))
                    u["o"] = 4 + (u["idx"] % 2)
                    u["us"] = u["idx"] % 2
                    if half == 0:
                        u.update(vsl=slice(64, 129), orows=slice(0, 65), dr=64, M=64)
                    else:
                        u.update(vsl=slice(0, 128), orows=slice(0, 128), dr=0, M=128)
                    u["hrows"] = slice(u["p0"], u["p0"] + 64)
                    u["kvar"] = u["kv"] * 2 + half
                    units.append(u)
        pairs = [(u, j) for u in units for j in range(u["npair"])]

        def emitS(i):
            u, j = pairs[i]
            sp_ = i % 2
            nq, q0, c = u["nq"], u["q0"], u["c"]
            fns = []
            for t_ in range(2):
                kt = u["keys"][2 * j + t_]
                fns.append(MM(pbig[sp_][:, t_ * 512:t_ * 512 + nq], kT[:, u["kvar"], kt * 128:(kt + 1) * 128], qT[:, c, q0:q0 + nq]))
            R.grp("pe", fns, reads=[], writes=[pbufs[2 * sp_], pbufs[2 * sp_ + 1]])

        def emitE(i):
            u, j = pairs[i]
            sp_ = i % 2
            pp_ = i % 3
            nq = u["nq"]
            if nq == 512:
                R.op("act", ACT(Pb[pp_][:, :], pbig[sp_][:, :], AF.Exp, scale=0.125),
                     reads=[pbufs[2 * sp_], pbufs[2 * sp_ + 1]], writes=[b_P[pp_]])
            else:
                R.op("act", ACT(v3(Pb[pp_][:, :], 512)[:, :, 0:nq], v3(pbig[sp_][:, :], 512)[:, :, 0:nq], AF.Exp, scale=0.125),
                     reads=[pbufs[2 * sp_], pbufs[2 * sp_ + 1]], writes=[b_P[pp_]])

        def emitPV(i):
            u, j = pairs[i]
            pp_ = i % 3
            nq, o = u["nq"], u["o"]
            fns = []
            for t_ in range(2):
                kt = u["keys"][2 * j + t_]
                fns.append(MM(bank(o)[u["orows"], 0:nq], VB[:, kt, u["kv"], u["vsl"]], Pb[pp_][:, t_ * 512:t_ * 512 + nq],
                              start=(j == 0 and t_ == 0), stop=(j == u["npair"] - 1 and t_ == 1)))
            R.grp("pe", fns, reads=[b_P[pp_]], writes=[pbufs[o]])

        def mk_norm(u):
            us, o, dr, M, hrows, asl, nq, half, c, q0 = (u[k_] for k_ in ("us", "o", "dr", "M", "hrows", "asl", "nq", "half", "c", "q0"))

            def rcp():
                R.op("dve", RCP(REC[us][dr:dr + 1, 0:nq], bank(o)[dr:dr + 1, 0:nq]), reads=[pbufs[o]], writes=[b_REC[us]])

            def norm():
                R.grp("pe", [MM(bank(6)[0:M, 0:nq], (E64f if dr == 64 else E0f)[:, 0:M], REC[us][:, 0:nq])],
                      reads=[b_REC[us], b_sel], writes=[pbufs[6]])
                R.op("dve", CP(BCS[us][hrows, 0:nq], bank(6)[hrows, 0:nq]), reads=[pbufs[6]], writes=[b_BCS[us]])
                R.op("dve", TT(ATT[asl][hrows, 0:nq], bank(o)[hrows, 0:nq], BCS[us][hrows, 0:nq], ALU.mult),
                     reads=[pbufs[o], b_BCS[us]], writes=[b_ATT[asl]])
                if half == 1:
                    R.dma("sp", d_mixT[c, :, q0:q0 + nq], ATT[asl][:, 0:nq], reads=[b_ATT[asl]])
            norm.rcp = rcp
            return norm

        pending = []
        npairs = len(pairs)
        emitS(0)
        if npairs > 1:
            emitS(1)
        for i, (u, j) in enumerate(pairs):
            npair = u["npair"]
            emitE(i)
            if i + 2 < npairs:
                emitS(i + 2)
            emitPV(i)
            if j == min(5, npair - 1) and pending:
                pending.pop()()
            if j in (1, 4, 7, 10, 13) or (npair < 6 and j == npair - 1):
                l_slot()
            if j == npair - 1:
                nf_ = mk_norm(u)
                nf_.rcp()
                pending.append(nf_)
                if u["idx"] % 3 == 0:
                    bg_tick()
        while pending:
            pending.pop()()
        if last:
            while bgq:
                bg_tick()
        else:
            while bgq:
                bg_tick()
        nflush = 0
        while l_slot():
            nflush += 1
        R.barrier()
        SB.release(lmark)

        WD = SB.alloc([128, NF, D], BF16)
        WO = SB.alloc([128, KC, D], BF16)
        b_w3 = Buf()
        R.dma("sp", WO[:, :, :], d_woutb[l].rearrange("(k p) n -> p k n", p=128), reads=[b_w3b[l]], writes=[b_w3])
        b_wd = Buf()
        wd_loads = []
        for f0 in range(0, NF, 4):
            f1 = min(NF, f0 + 4)
            wd_loads.append(lambda f0=f0, f1=f1: R.dma("sp", WD[:, f0:f1, :], d_wdnb[l, f0 * 128:f1 * 128, :].rearrange("(f p) n -> p f n", p=128),
                                                     reads=[b_w3b[l]], writes=[b_wd]))
        fcw = SB.alloc([128, 132], F32)
        R.dma("sp", fcw[:], d_fcw[l].rearrange("p c k -> p (c k)"), writes=[b_w3])
        GA, GFm, SFm, GFg = (SB.alloc([128, D], F32) for _ in range(4))
        b_rep3 = Buf()
        GFIN = None
        if last:
            GFIN = SB.alloc([128, D], F32)
            R.dma("sp", GFIN[:], d_gfin[0:1, :].partition_broadcast(128), writes=[b_w3])
        TMP3 = [SB.alloc([128, D], F32) for _ in range(2)]
        b_TMP3 = [Buf() for _ in range(2)]

        def load_reps_b(row):
            load_rep(GA, l, row, 2, b_rep3)
            load_rep(SFm, l, row, 3, b_rep3)
            load_rep(GFm, l, row, 4, b_rep3)
            R.dma("sp", TMP3[0][:], d_gffn[l:l + 1, :].partition_broadcast(128), writes=[b_TMP3[0]])
            R.op("dve", TT(GFm[:], GFm[:], TMP3[0][:], ALU.mult), reads=[b_rep3, b_TMP3[0]], writes=[b_rep3])

        def load_reps_g(row):
            load_rep(GFg, l, row, 5, b_repg)

        b_repg = Buf()
        NWU = 4
        WU = [SB.alloc([128, 2048], BF16) for _ in range(NWU)]
        b_WU = [Buf() for _ in range(NWU)]
        NXW = 8
        XW = [SB.alloc([128, D], F32) for _ in range(NXW)]
        b_XW = [Buf() for _ in range(NXW)]
        for j in range(NXW):
            R.op("dve", MS(XW[j][:], 0.0), writes=[b_XW[j]])
        H2 = [SB.alloc([128, D], BF16) for _ in range(2)]
        b_H2 = [Buf() for _ in range(2)]
        st3 = [SB.alloc([128, 16], F32) for _ in range(2)]
        b_st3 = [Buf() for _ in range(2)]
        H2T = [SB.alloc([128, KC, 512], BF16) for _ in range(2)]
        b_H2T = [[Buf() for _ in range(4)] for _ in range(2)]
        MW = [SB.alloc([128, KC, 512], BF16) for _ in range(2)]
        b_MW = [Buf() for _ in range(2)]
        for j in range(2):
            R.op("dve", MS(MW[j][:], 0.0), writes=[b_MW[j]])
        AT = SB.alloc([128, NF, 512], BF16)
        b_AT = Buf()
        R.op("pool", MS(AT[:], 0.0), writes=[b_AT])
        NG = 2
        CUb = [SB.alloc([128, 512], F32) for _ in range(NG)]
        G0 = [SB.alloc([128, 512], F32) for _ in range(NG)]
        G1 = [SB.alloc([128, 512], F32) for _ in range(NG)]
        G2 = [SB.alloc([128, 512], F32) for _ in range(NG)]
        b_CU, b_G0, b_G1, b_G2 = ([Buf() for _ in range(NG)] for _ in range(4))

        windows = []
        if not last:
            windows.append((0, CTX, 0, CTX, 1))
        s_ = CTX
        while s_ < T:
            n_ = min(510, T - s_)
            windows.append((CTX, T, s_, n_, 0))
            s_ += n_
        st = {"row": None, "xw": 0, "pj": 0, "tr": 0, "up": 0, "wu": 0, "tk": 0, "g": 0}
        winfo = {}

        def proj_half(fn_mm, m):
            pj = st["pj"] % 2
            st["pj"] += 1
            return pj

        def stageBL_list(wi):
            seg_lo, seg_hi, s, n, mrow = windows[wi]
            W = n + 2
            wlo = s - 1
            tiles = [(c0, min(128, W - c0)) for c0 in range(0, W, 128)]
            ms = wi % 2
            hsl = wi % 2
            v0 = max(wlo, seg_lo)
            v1 = min(wlo + W, seg_hi)
            out = []
            for k0 in (0, 4):
                out.append(lambda k0=k0: R.dma("sp", MW[ms][:, k0:k0 + 4, v0 - wlo:v1 - wlo],
                                               d_mixT[k0:k0 + 4, :, v0:v1].rearrange("k p t -> p k t"), writes=[b_MW[ms]]))
            xslots = []
            for ti_, (c0, m) in enumerate(tiles):
                xs = st["xw"] % NXW
                st["xw"] += 1
                xslots.append(xs)
                r0 = max(wlo + c0, seg_lo)
                r1 = min(wlo + c0 + m, seg_hi)
                p0_ = r0 - (wlo + c0)
                if r1 > r0:
                    out.append(lambda xs=xs, p0_=p0_, r0=r0, r1=r1: R.dma("sp", XW[xs][p0_:p0_ + (r1 - r0), :], seq_in[r0:r1, :],
                                                                         writes=[b_XW[xs]]))
            winfo[wi] = (W, wlo, tiles, xslots, hsl)
            return out

        def stageBL(wi):
            for f_ in stageBL_list(wi):
                f_()

        def stageB_steps(wi):
            seg_lo, seg_hi, s, n, mrow = windows[wi]
            W, wlo, tiles, xslots, hsl = winfo[wi]
            ms = wi % 2
            ops, trs = [], []
            tsl = {}

            def mk_op(ti_, c0, m):
                def op():
                    if ti_ == 0 and st["row"] != mrow:
                        load_reps_b(mrow)
                        st["row"] = mrow
                    xs = xslots[ti_]
                    ts = st["tk"] % 2
                    st["tk"] += 1
                    tsl[ti_] = ts
                    for hf in range(2):
                        pj = st["pj"] % 2
                        st["pj"] += 1
                        hc = slice(hf * 512, (hf + 1) * 512)
                        fns = [MM(bank(pj)[0:m, :], MW[ms][:, kc, c0:c0 + m], WO[:, kc, hc], start=(kc == 0), stop=(kc == KC - 1))
                               for kc in range(KC)]
                        R.grp("pe", fns, reads=[b_MW[ms], b_w3], writes=[pbufs[pj]])
                        R.op("dve", TT(TMP3[ts][0:m, hc], bank(pj)[0:m, :], GA[0:m, hc], ALU.mult),
                             reads=[pbufs[pj], b_rep3], writes=[b_TMP3[ts]])
                    R.op("dve", TT(XW[xs][0:m, :], XW[xs][0:m, :], TMP3[ts][0:m, :], ALU.add), reads=[b_TMP3[ts], b_XW[xs]], writes=[b_XW[xs]])
                    R.op("act", ACT(TMP3[ts][:], XW[xs][:], AF.Square, accum_out=st3[ts][:, 0:1]),
                         reads=[b_XW[xs]], writes=[b_TMP3[ts], b_st3[ts]])
                    R.op("dve", TS(st3[ts][:, 1:2], st3[ts][:, 0:1], 1.0 / D, EPS, ALU.mult, ALU.add), reads=[b_st3[ts]], writes=[b_st3[ts]])
                    R.op("pool", TT(st3[ts][:, 2:3], st3[ts][:, 1:2], neghalf[:, 0:1], ALU.pow), reads=[b_st3[ts]], writes=[b_st3[ts]])
                    R.op("dve", STT(TMP3[ts][:], XW[xs][:], st3[ts][:, 2:3], GFm[:], ALU.mult, ALU.mult),
                         reads=[b_XW[xs], b_st3[ts], b_rep3], writes=[b_TMP3[ts]])
                    R.op("pool", TT(H2[ts][:], TMP3[ts][:], SFm[:], ALU.add), reads=[b_TMP3[ts], b_rep3], writes=[b_H2[ts]])
                return op

            def mk_tr(ti_, c0, m):
                def tr():
                    ts = tsl[ti_]
                    tb = (2, 7)[st["tr"] % 2]
                    st["tr"] += 1
                    fns = [TR(bank_bf(tb)[:, kc * 128:(kc + 1) * 128], H2[ts][:, kc * 128:(kc + 1) * 128], ident[:]) for kc in range(KC)]
                    R.grp("pe", fns, reads=[b_H2[ts], b_const], writes=[pbufs[tb]])
                    R.op("act", ACT(H2T[hsl][:, :, c0:c0 + m], v3(bank_bf(tb), 128)[:, :, 0:m], AF.Copy), reads=[pbufs[tb]], writes=[b_H2T[hsl][ti_]])
                    if ti_ == 0 and wlo < seg_lo:
                        R.op("pool", MS(H2T[hsl][:, :, 0:1], 0.0), reads=[], writes=[b_H2T[hsl][0]])
                    if ti_ == len(tiles) - 1 and wlo + W > seg_hi:
                        R.op("pool", MS(H2T[hsl][:, :, W - 1:W], 0.0), reads=[], writes=[b_H2T[hsl][len(tiles) - 1]])
                return tr

            for ti_, (c0, m) in enumerate(tiles):
                ops.append(mk_op(ti_, c0, m))
                trs.append(mk_tr(ti_, c0, m))
            return ops, trs

        def run_B_interleaved(wi, fillers):
            ops, trs = stageB_steps(wi)
            nt = len(ops)
            fill = list(fillers)
            for i in range(nt):
                ops[i]()
                if i >= 1:
                    trs[i - 1]()
            if fill:
                fill.pop(0)()
            trs[nt - 1]()
            for f_ in fill:
                f_()

        def stageUp(wi, mid=None):
            W, wlo, tiles, xslots, hsl = winfo[wi]
            upend = []
            nv = W - 2
            hdeps = b_H2T[hsl][0:len(tiles)]
            midl = mid() if mid is not None else []
            for f in range(NF):
                if f % 2 == 1 and wd_loads:
                    wd_loads.pop(0)()
                if f >= 4 and f % 2 == 0 and midl:
                    midl.pop(0)()
                ws_ = st["wu"] % NWU
                st["wu"] += 1
                R.dma("sp", WU[ws_][:], d_wupb[l, f * 128:(f + 1) * 128, :], reads=[b_wupb], writes=[b_WU[ws_]])
                bu = 3 + 2 * (st["up"] % 2)
                bg = bu + 1
                st["up"] += 1
                gs = st["g"] % NG
                st["g"] += 1
                fns = [MM(bank(bu)[:, 0:W], WU[ws_][:, kc * 256:kc * 256 + 128], H2T[hsl][:, kc, 0:W], start=(kc == 0), stop=(kc == KC - 1))
                       for kc in range(KC)]
                R.grp("pe", fns, reads=[b_WU[ws_]] + hdeps, writes=[pbufs[bu]])
                fns = [MM(bank(bg)[:, 0:W], WU[ws_][:, kc * 256 + 128:kc * 256 + 256], H2T[hsl][:, kc, 0:W], start=(kc == 0), stop=(kc == KC - 1))
                       for kc in range(KC)]
                R.grp("pe", fns, reads=[b_WU[ws_]] + hdeps, writes=[pbufs[bg]])
                cu_ = f * 3
                cg_ = (NF + f) * 3
                R.op("dve", TS(CUb[gs][:, 0:nv], bank(bu)[:, 0:nv], fcw[:, cu_:cu_ + 1], None, ALU.mult),
                     reads=[pbufs[bu], b_w3], writes=[b_CU[gs]])
                for k in (1, 2):
                    R.op("dve", STT(CUb[gs][:, 0:nv], bank(bu)[:, k:k + nv], fcw[:, cu_ + k:cu_ + k + 1], CUb[gs][:, 0:nv], ALU.mult, ALU.add),
                         reads=[pbufs[bu], b_w3, b_CU[gs]], writes=[b_CU[gs]])
                Gk = (G0, G1, G2)
                bGk = (b_G0, b_G1, b_G2)
                for k in range(3):
                    R.op("act", ACT(Gk[k][gs][:, 0:nv], bank(bg)[:, k:k + nv], AF.Copy, scale=fcw[:, cg_ + k:cg_ + k + 1]),
                         reads=[pbufs[bg], b_w3], writes=[bGk[k][gs]])
                R.op("pool", TT(G0[gs][:, 0:nv], G0[gs][:, 0:nv], G1[gs][:, 0:nv], ALU.add),
                     reads=[b_G0[gs], b_G1[gs]], writes=[b_G0[gs]])
                R.op("pool", TT(G0[gs][:, 0:nv], G0[gs][:, 0:nv], G2[gs][:, 0:nv], ALU.add),
                     reads=[b_G0[gs], b_G2[gs]], writes=[b_G0[gs]])

                def fin(gs=gs, f=f, nv=nv):
                    R.op("act", ACT(G1[gs][:, 0:nv], G0[gs][:, 0:nv], AF.Silu), reads=[b_G0[gs]], writes=[b_G1[gs]])
                    R.op("dve", TT(AT[:, f, 1:1 + nv], G1[gs][:, 0:nv], CUb[gs][:, 0:nv], ALU.mult),
                         reads=[b_G1[gs], b_CU[gs]], writes=[b_AT])
                if upend:
                    upend.pop()()
                upend.append(fin)
            while upend:
                upend.pop()()
            while midl:
                midl.pop(0)()

        def down_steps(wi):
            W, wlo, tiles, xslots, hsl = winfo[wi]
            mrow_d = windows[wi][4]
            steps = []

            def mk(ti_, c0, m):
                def dn():
                    if ti_ == 0 and st.get("rowg") != mrow_d:
                        load_reps_g(mrow_d)
                        st["rowg"] = mrow_d
                    xs = xslots[ti_]
                    ts = st["tk"] % 2
                    st["tk"] += 1
                    for hf in range(2):
                        pj = st["pj"] % 2
                        st["pj"] += 1
                        hc = slice(hf * 512, (hf + 1) * 512)
                        fns = [MM(bank(pj)[0:m, :], AT[:, f, c0:c0 + m], WD[:, f, hc], start=(f == 0), stop=(f == NF - 1)) for f in range(NF)]
                        R.grp("pe", fns, reads=[b_AT, b_w3, b_wd], writes=[pbufs[pj]])
                        R.op("dve", TT(TMP3[ts][0:m, hc], bank(pj)[0:m, :], GFg[0:m, hc], ALU.mult),
                             reads=[pbufs[pj], b_repg], writes=[b_TMP3[ts]])
                    R.op("dve", TT(XW[xs][0:m, :], XW[xs][0:m, :], TMP3[ts][0:m, :], ALU.add), reads=[b_TMP3[ts], b_XW[xs]], writes=[b_XW[xs]])
                    a0 = max(c0, 1)
                    a1 = min(c0 + m, W - 1)
                    if a1 <= a0:
                        return
                    pa0 = a0 - c0
                    npart = a1 - a0
                    row0 = wlo + a0
                    if not last:
                        R.dma("pool", d_res[row0:row0 + npart, :], XW[xs][pa0:pa0 + npart, :], reads=[b_XW[xs]])
                    else:
                        R.op("act", ACT(TMP3[ts][:], XW[xs][:], AF.Square, accum_out=st3[ts][:, 0:1]),
                             reads=[b_XW[xs]], writes=[b_TMP3[ts], b_st3[ts]])
                        R.op("dve", TS(st3[ts][:, 1:2], st3[ts][:, 0:1], 1.0 / D, EPS, ALU.mult, ALU.add), reads=[b_st3[ts]], writes=[b_st3[ts]])
                        R.op("pool", TT(st3[ts][:, 2:3], st3[ts][:, 1:2], neghalf[:, 0:1], ALU.pow), reads=[b_st3[ts]], writes=[b_st3[ts]])
                        R.op("dve", STT(XW[xs][:], XW[xs][:], st3[ts][:, 2:3], GFIN[:], ALU.mult, ALU.mult),
                             reads=[b_XW[xs], b_st3[ts], b_w3], writes=[b_XW[xs]])
                        R.dma("pool", d_y[row0 - CTX:row0 - CTX + npart, :], XW[xs][pa0:pa0 + npart, :], reads=[b_XW[xs]])
                return dn
            for ti_, (c0, m) in enumerate(tiles):
                steps.append(mk(ti_, c0, m))
            return steps

        nwin = len(windows)
        wi = 0
        stageBL(wi)
        run_B_interleaved(wi, [])
        while wi < nwin:
            if wi + 1 < nwin:
                stageUp(wi, mid=lambda w=wi + 1: stageBL_list(w))
            else:
                stageUp(wi)
            dsteps = down_steps(wi)
            if wi + 1 < nwin:
                run_B_interleaved(wi + 1, dsteps)
            else:
                for d_ in dsteps:
                    d_()
            wi += 1
        R.barrier()
        SB.release(lmark)

    for l in range(DEPTH):
        layer(l)
    R.barrier()
    R.emit(nc)
    return nc, SB.peak


def _host_prep(inp):
    f = np.float32
    g = {k: np.asarray(v) for k, v in inp.items()}
    L = DEPTH
    w_in = g["w_in"]
    q = w_in[:, :, 0:512]
    k0 = w_in[:, :, 512:576]
    k1 = w_in[:, :, 576:640]
    v = w_in[:, :, 640:768]
    rest = w_in[:, :, 768:2048]
    w_in_e = np.ascontiguousarray(np.concatenate([q, k0, k0, k1, k1, v, rest], axis=2), dtype=f)
    gqk = np.ascontiguousarray(np.concatenate([np.tile(g["g_q"], (1, 8)), np.tile(g["g_k"], (1, 4))], axis=1), dtype=f)
    rows = SEQ // 64
    pos_r = np.repeat(np.arange(rows, dtype=f), 64)
    pos_c = np.tile(np.arange(64, dtype=f), rows)
    inv = (f(10000.0) ** (-np.arange(16, dtype=f) / f(16))).astype(f)
    ang = np.concatenate([pos_r[:, None] * inv, pos_c[:, None] * inv], axis=-1).astype(f)
    cos = np.cos(ang).astype(f)
    sin = np.sin(ang).astype(f)
    cs = np.zeros((T, 128), f)
    cs[:CTX, 0:64] = 1.0
    cs[CTX:, 0:32] = cos
    cs[CTX:, 32:64] = cos
    cs[CTX:, 64:96] = -sin
    cs[CTX:, 96:128] = sin
    lru_cw = np.ascontiguousarray(g["lru_conv_w"].reshape(L, 4, 2, 128).transpose(0, 3, 2, 1), dtype=f)
    lru_cb = np.ascontiguousarray(g["lru_conv_b"].reshape(L, 2, 128).transpose(0, 2, 1), dtype=f)
    wbd = np.zeros((L, 2, 2, 2, 128, 128), f)
    for gi, nm in enumerate(("lru_wa", "lru_wi")):
        w = g[nm]
        for c in range(2):
            for j in range(2):
                wbd[:, :, gi, c, j * 64:(j + 1) * 64, j * 64:(j + 1) * 64] = w[:, :, 2 * c + j]
    lvec = np.stack([g["lru_ba"], g["lru_bi"], g["lru_lam"]], axis=-1)
    lvec = np.ascontiguousarray(lvec.reshape(L, 2, 2, 128, 3).transpose(0, 3, 1, 2, 4), dtype=f)
    sc_cw = np.ascontiguousarray(g["sc_conv_w"].reshape(L, 3, 2, 128).transpose(0, 3, 2, 1), dtype=f)
    wu = g["w_up"].reshape(L, KC, 128, 2, NF, 128)
    wup_r = np.ascontiguousarray(wu.transpose(0, 4, 2, 1, 3, 5).reshape(L, NF * 128, 2048), dtype=f)
    fcw = np.ascontiguousarray(g["ffn_conv_w"].reshape(L, 3, 44, 128).transpose(0, 3, 2, 1), dtype=f)
    shared = {
        "w_mod": np.ascontiguousarray(g["w_mod"], dtype=f), "b_mod": np.ascontiguousarray(g["b_mod"], dtype=f),
        "g_mix": np.ascontiguousarray(g["g_mix"], dtype=f), "g_ffn": np.ascontiguousarray(g["g_ffn"], dtype=f),
        "g_final": np.ascontiguousarray(g["g_final"].reshape(1, D), dtype=f),
        "w_in_e": w_in_e, "gqk": gqk, "cs": cs, "lru_cw": lru_cw, "lru_cb": lru_cb, "wbd": wbd, "lru_vec": lvec,
        "sc_cw": sc_cw, "w_out": np.ascontiguousarray(g["w_out"], dtype=f), "wup_r": wup_r, "fcw": fcw,
        "w_down": np.ascontiguousarray(g["w_down"], dtype=f), "ident": np.eye(128, dtype=f),
    }
    maps = []
    for b in range(8):
        m = dict(shared)
        m["xin"] = np.ascontiguousarray(np.concatenate([g["ctx"][b], g["x"][b]], axis=0), dtype=f)
        cv = np.stack([g["c"][b], g["c_ctx"]], axis=0)
        m["cT"] = np.ascontiguousarray(cv.reshape(2, KC, 128).transpose(2, 1, 0).reshape(128, 16), dtype=f)
        maps.append(m)
    return maps


_NC_CACHE = {}


def kernel(**inputs):
    maps = _host_prep(inputs)
    if "nc" not in _NC_CACHE:
        _NC_CACHE["nc"] = build_nc()[0]
    nc = _NC_CACHE["nc"]
    res = run_bass_kernel_spmd(nc, maps, core_ids=list(range(8)))
    out = np.stack([np.asarray(r["y"], dtype=np.float32) for r in res.results], axis=0)
    return out
```

```python
import numpy as np
import concourse.bass as bass
import concourse.mybir as mybir
from concourse.bass_utils import run_bass_kernel_spmd

F32 = mybir.dt.float32
BF16 = mybir.dt.bfloat16
AF = mybir.ActivationFunctionType
ALU = mybir.AluOpType
AX = mybir.AxisListType

CTX = 256
SEQ = 4096
T = CTX + SEQ
NT = T // 128
D = 1024
KC = 8
DFF = 2816
NF = 22
WIN_E = 2176
EPS = 1e-6
PADW = 4360
SB_BASE = 8192 + 1024
SB_LIMIT = 8192 + 220900
DEPTH = 2
ROLL = 12000

ENGS = ("pe", "act", "dve", "pool", "sp")


def pcol(tok):
    return tok + 2 if tok < CTX else tok + 4


def MM(out, lhsT, rhs, start=True, stop=True):
    return lambda e: e.matmul(out, lhsT=lhsT, rhs=rhs, start=start, stop=stop)


def TR(out, in_, identity):
    return lambda e: e.transpose(out=out, in_=in_, identity=identity)


def ACT(out, in_, func, **kw):
    return lambda e: e.activation(out=out, in_=in_, func=func, **kw)


def TT(out, in0, in1, op):
    return lambda e: e.tensor_tensor(out=out, in0=in0, in1=in1, op=op)


def TS(out, in0, s1, s2, op0, op1=None):
    if op1 is None:
        return lambda e: e.tensor_scalar(out=out, in0=in0, scalar1=s1, scalar2=None, op0=op0)
    return lambda e: e.tensor_scalar(out=out, in0=in0, scalar1=s1, scalar2=s2, op0=op0, op1=op1)


def STT(out, in0, scalar, in1, op0, op1):
    return lambda e: e.scalar_tensor_tensor(out=out, in0=in0, scalar=scalar, in1=in1, op0=op0, op1=op1)


def RCP(out, in_):
    return lambda e: e.reciprocal(out=out, in_=in_)


def CP(out, in_):
    return lambda e: e.tensor_copy(out=out, in_=in_)


def MS(ap, v):
    return lambda e: e.memset(ap, v)


def RED(out, in_, axis, op):
    return lambda e: e.tensor_reduce(out=out, in_=in_, axis=axis, op=op)


def SCAN(out, d0, d1, init):
    return lambda e: e.tensor_tensor_scan(out=out, data0=d0, data1=d1, initial=init, op0=ALU.mult, op1=ALU.add)


class Buf:
    __slots__ = ("w", "r")

    def __init__(self):
        self.w = None
        self.r = []


class Rec:
    def __init__(self):
        self.q = {e: [] for e in ENGS}
        self.cnt = {e: 0 for e in ENGS}
        self.waited = {e: {} for e in ENGS}
        self.ndsem = 8
        self.dq = {"sp": [0] * self.ndsem, "pool": [0] * self.ndsem, "act": [0] * self.ndsem}
        self.dnext = {"sp": 0, "pool": 0, "act": 0}
        self.last_d = {}

    def _deps(self, reads, writes):
        deps = []
        for b in reads:
            if b.w is not None:
                deps.append(b.w)
        for b in writes:
            if b.w is not None:
                deps.append(b.w)
            deps.extend(b.r)
        return deps

    def _waits(self, eng, deps):
        out = []
        wd = self.waited[eng]
        for t in deps:
            if t[0] == "E":
                if t[1] == "pe" and eng == "pe":
                    continue
                key = ("E", t[1])
            else:
                key = ("D", t[1], t[2])
            v = t[-1]
            if wd.get(key, 0) >= v:
                continue
            wd[key] = v
            out.append(t)
        best = {}
        for t in out:
            key = t[:-1]
            if key not in best or best[key][-1] < t[-1]:
                best[key] = t
        return list(best.values())

    def _post(self, tok, reads, writes):
        for b in writes:
            b.w = tok
            b.r = []
        for b in reads:
            b.r.append(tok)

    def grp(self, eng, fns, reads=(), writes=()):
        waits = self._waits(eng, self._deps(reads, writes))
        self.cnt[eng] += 1
        tok = ("E", eng, self.cnt[eng])
        self.q[eng].append((waits, fns, tok))
        self._post(tok, reads, writes)
        return tok

    def op(self, eng, fn, reads=(), writes=()):
        return self.grp(eng, [fn], reads, writes)

    def dma(self, qe, out_ap, in_ap, reads=(), writes=()):
        deps = self._deps(reads, writes)
        i = self.dnext[qe]
        self.dnext[qe] = (i + 1) % self.ndsem
        prev = self.dq[qe][i]
        if prev > 0:
            deps.append(("D", qe, i, prev))
        waits = self._waits(qe, deps)
        val = prev + 16
        self.dq[qe][i] = val
        tok = ("D", qe, i, val)
        self.q[qe].append((waits, [lambda e, o=out_ap, a=in_ap: e.dma_start(out=o, in_=a)], tok))
        self._post(tok, reads, writes)
        self.last_d[(qe, i)] = tok
        return tok

    def barrier(self):
        toks = []
        for e in ENGS:
            if self.cnt[e] > 0:
                toks.append(("E", e, self.cnt[e]))
        toks.extend(self.last_d.values())
        for e in ENGS:
            wd = self.waited[e]
            ws = []
            for t in toks:
                if t[0] == "E":
                    key = ("E", t[1])
                else:
                    key = ("D", t[1], t[2])
                if wd.get(key, 0) >= t[-1]:
                    continue
                wd[key] = t[-1]
                ws.append(t)
            if ws:
                self.q[e].append((ws, [], None))

    def emit(self, nc):
        from contextlib import ExitStack
        with ExitStack() as es:
            esem = {}
            for e in ENGS:
                n = (self.cnt[e] + ROLL - 1) // ROLL + 1
                esem[e] = [es.enter_context(nc.semaphore(f"s_{e}_{k}")) for k in range(n)]
            dsem = {}
            for qe in ("sp", "pool", "act"):
                dsem[qe] = [es.enter_context(nc.semaphore(f"d_{qe}_{k}")) for k in range(self.ndsem)]
            block = es.enter_context(nc.Block())

            def dowait(e, t):
                if t[0] == "E":
                    c = t[2]
                    e.wait_ge(esem[t[1]][(c - 1) // ROLL], (c - 1) % ROLL + 1)
                else:
                    e.wait_ge(dsem[t[1]][t[2]], t[3])

            def replay(name, e):
                for waits, fns, tok in self.q[name]:
                    for t in waits:
                        dowait(e, t)
                    ins = None
                    for f in fns:
                        ins = f(e)
                    if tok is not None and ins is not None:
                        if tok[0] == "E":
                            c = tok[2]
                            ins.then_inc(esem[name][(c - 1) // ROLL], 1)
                        else:
                            ins.then_inc(dsem[tok[1]][tok[2]], 16)

            @block.tensor
            def _(e):
                replay("pe", e)

            @block.scalar
            def _(e):
                replay("act", e)

            @block.vector
            def _(e):
                replay("dve", e)

            @block.gpsimd
            def _(e):
                replay("pool", e)

            @block.sync
            def _(e):
                replay("sp", e)


class SBAlloc:
    def __init__(self, nc):
        self.nc = nc
        self.off = SB_BASE
        self.n = 0
        self.peak = 0

    def mark(self):
        return self.off

    def release(self, m):
        self.off = m

    def alloc(self, shape, dtype):
        esz = 4 if dtype == F32 else 2
        nbytes = int(np.prod(shape[1:])) * esz
        nbytes = (nbytes + 63) // 64 * 64
        assert self.off + nbytes <= SB_LIMIT, f"SBUF overflow {self.off + nbytes}"
        self.n += 1
        t = self.nc.alloc_sbuf_tensor_at(f"sb{self.n}", list(shape), dtype, offset=self.off)
        self.off += nbytes
        self.peak = max(self.peak, self.off)
        return t


def build_nc(debug=False):
    nc = bass.Bass("TRN2", target_bir_lowering=False, dynamic_dma_scratch_size=8192)
    R = Rec()
    SB = SBAlloc(nc)

    def din(name, shape, dt=F32):
        return nc.dram_tensor(name, list(shape), dt, kind="ExternalInput").ap()

    def dscr(name, shape, dt=F32, out=False):
        if out:
            return nc.dram_tensor(name, list(shape), dt, kind="ExternalOutput").ap()
        return nc.dram_tensor(name, list(shape), dt).ap()

    d_xin = din("xin", [T, D])
    d_cT = din("cT", [128, 16])
    d_wmod = din("w_mod", [DEPTH, D, 6 * D])
    d_bmod = din("b_mod", [DEPTH, 6 * D])
    d_gmix = din("g_mix", [DEPTH, D])
    d_gffn = din("g_ffn", [DEPTH, D])
    d_gfin = din("g_final", [1, D])
    d_win = din("w_in_e", [DEPTH, D, WIN_E])
    d_gqk = din("gqk", [DEPTH, 768])
    d_cs = din("cs", [T, 128])
    d_lcw = din("lru_cw", [DEPTH, 128, 2, 4])
    d_lcb = din("lru_cb", [DEPTH, 128, 2])
    d_wbd = din("wbd", [DEPTH, 2, 2, 2, 128, 128])
    d_lvec = din("lru_vec", [DEPTH, 128, 2, 2, 3])
    d_scw = din("sc_cw", [DEPTH, 128, 2, 3])
    d_wout = din("w_out", [DEPTH, D, D])
    d_wup = din("wup_r", [DEPTH, NF * 128, 2048])
    d_fcw = din("fcw", [DEPTH, 128, 44, 3])
    d_wdn = din("w_down", [DEPTH, DFF, D])
    d_ident = din("ident", [128, 128])
    d_y = nc.dram_tensor("y", [SEQ, D], F32, kind="ExternalOutput").ap()

    d_modd = dscr("modd", [DEPTH, 2, 6 * D], out=debug)
    d_res = dscr("res", [T, D], out=debug)
    d_rT = dscr("rT", [2, 128, T], out=debug)
    d_gT = dscr("gT", [2, 128, T])
    d_bT = dscr("bT", [2, 128, T])
    d_cuT = dscr("cuT", [2, 128, T])
    d_mixT = dscr("mixT", [8, 128, T], BF16, out=debug)
    d_wupb = dscr("wupb", [DEPTH, NF * 128, 2048], BF16)
    d_winb = dscr("winb", [DEPTH, D, WIN_E], BF16)
    d_woutb = dscr("woutb", [DEPTH, D, D], BF16)
    d_wdnb = dscr("wdnb", [DEPTH, DFF, D], BF16)

    pbig = [nc.alloc_psum_tensor(f"pb{j}", [128, 1024], F32) for j in range(4)]

    def bank(i):
        return pbig[i // 2][:, (i % 2) * 512:(i % 2) * 512 + 512]

    def bank_bf(i):
        return bank(i).bitcast(BF16)

    pbufs = [Buf() for _ in range(8)]

    ident = SB.alloc([128, 128], BF16)
    ones_f = SB.alloc([128, 128], F32)
    neghalf = SB.alloc([128, 16], F32)
    b_const = Buf()
    R.dma("pool", ident[:], d_ident[:, :], writes=[b_const])
    R.op("pool", MS(ones_f[:], 1.0), writes=[b_const])
    R.op("pool", MS(neghalf[:], -0.5), writes=[b_const])
    b_wupb = Buf()
    b_winb = Buf()
    b_w3b = [Buf() for _ in range(DEPTH)]
    base_mark = SB.mark()

    def phase0():
        cT = SB.alloc([128, 16], F32)
        scT = SB.alloc([128, 16], BF16)
        bm = SB.alloc([2, 6 * D], F32)
        m2 = SB.alloc([2, 6 * D], F32)
        NW = 6
        wm = [SB.alloc([128, 3072], BF16) for _ in range(NW)]
        b_wm = [Buf() for _ in range(NW)]
        b_cT, b_scT, b_bm, b_m2 = Buf(), Buf(), Buf(), Buf()
        R.dma("sp", cT[:], d_cT[:, :], writes=[b_cT])
        R.op("act", ACT(scT[:], cT[:], AF.Silu), reads=[b_cT], writes=[b_scT])
        k = 0
        for l in range(DEPTH):
            R.dma("sp", bm[:], d_bmod[l:l + 1, :].partition_broadcast(2), writes=[b_bm])
            for half in range(2):
                for kc in range(KC):
                    s = k % NW
                    k += 1
                    R.dma("pool", wm[s][:], d_wmod[l, kc * 128:(kc + 1) * 128, half * 3072:(half + 1) * 3072],
                          writes=[b_wm[s]])
                    fns = [MM(bank(cb)[0:2, :], scT[:, 2 * kc:2 * kc + 2], wm[s][:, cb * 512:(cb + 1) * 512],
                              start=(kc == 0), stop=(kc == KC - 1)) for cb in range(6)]
                    R.grp("pe", fns, reads=[b_scT, b_wm[s]], writes=pbufs[0:6])
                for cb in range(6):
                    c0 = half * 3072 + cb * 512
                    R.op("dve", TT(m2[0:2, c0:c0 + 512], bank(cb)[0:2, :], bm[0:2, c0:c0 + 512], ALU.add),
                         reads=[pbufs[cb], b_bm], writes=[b_m2])
            for c0 in (1024, 4096):
                R.op("dve", TS(m2[0:2, c0:c0 + 1024], m2[0:2, c0:c0 + 1024], 1.0, None, ALU.add), reads=[b_m2], writes=[b_m2])
            R.dma("sp", d_modd[l, :, :], m2[:], reads=[b_m2])

    phase0()
    R.barrier()
    SB.release(base_mark)

    bgq = []

    def emit_bg_casts():
        def add(dst, src, buf):
            bgq.append(lambda: R.dma("pool", dst, src, writes=[buf]))
        def pieces(dst, src, nrows, npc, buf):
            step = nrows // npc
            for part in range(npc):
                add(dst[part * step:(part + 1) * step, :], src[part * step:(part + 1) * step, :], buf)
        for l in range(DEPTH):
            if l > 0:
                pieces(d_winb[l], d_win[l], D, 4, b_winb)
            pieces(d_woutb[l], d_wout[l], D, 4, b_w3b[l])
            pieces(d_wdnb[l], d_wdn[l], DFF, 8, b_w3b[l])
            pieces(d_wupb[l], d_wup[l], NF * 128, 8, b_wupb)

    def bg_tick():
        if bgq:
            bgq.pop(0)()

    def load_rep(tile, l, row, seg, buf):
        R.dma("sp", tile[:], d_modd[l, row:row + 1, seg * D:(seg + 1) * D].partition_broadcast(128), writes=[buf])

    def v3(ap, d):
        return ap.rearrange("p (h d) -> p h d", d=d)

    def layer(l):
        last = (l == DEPTH - 1)
        seq_in = d_xin if l == 0 else d_res
        lmark = SB.mark()
        qT = SB.alloc([128, 4, T], BF16)
        kT = SB.alloc([128, 4, T], BF16)
        VB = SB.alloc([128, NT, 2, 129], BF16)
        b_qT = [Buf() for _ in range(NT)]
        b_VB = Buf()

        p1mark = SB.mark()
        win = SB.alloc([128, KC, WIN_E], BF16)
        b_win = Buf()
        for kc in range(KC):
            if l == 0:
                R.dma("pool", win[:, kc, :], d_win[l, kc * 128:(kc + 1) * 128, :], writes=[b_win])
            else:
                R.dma("sp", win[:, kc, :], d_winb[l, kc * 128:(kc + 1) * 128, :], reads=[b_winb], writes=[b_win])
        if l == 0:
            emit_bg_casts()
        GX, SX = (SB.alloc([128, D], F32) for _ in range(2))
        b_rep = Buf()

        def load_mix_reps(row):
            load_rep(GX, l, row, 1, b_rep)
            load_rep(SX, l, row, 0, b_rep)
            R.dma("sp", TMP[0][:], d_gmix[l:l + 1, :].partition_broadcast(128), writes=[b_TMP[0]])
            R.op("dve", TT(GX[:], GX[:], TMP[0][:], ALU.mult), reads=[b_rep, b_TMP[0]], writes=[b_rep])

        gqk = SB.alloc([128, 768], F32)
        R.dma("sp", gqk[:], d_gqk[l:l + 1, :].partition_broadcast(128), writes=[b_rep])
        R.op("pool", MS(kT[:], 0.0), writes=[b_VB])
        R.op("pool", MS(VB[:], 0.0), writes=[b_VB])
        R.op("pool", MS(VB[:, :, :, 0:1], 1.0), writes=[b_VB])
        R.op("pool", MS(VB[:, :, :, 128:129], 1.0), writes=[b_VB])

        NX = 2
        XS = [SB.alloc([128, D], F32) for _ in range(NX)]
        b_XS = [Buf() for _ in range(NX)]
        NTAB = 6
        TAB = [SB.alloc([128, 128], F32) for _ in range(NTAB)]
        b_TAB = [Buf() for _ in range(NTAB)]
        TMP = [SB.alloc([128, D], F32) for _ in range(2)]
        b_TMP = [Buf() for _ in range(2)]
        NH = 4
        Hb = [SB.alloc([128, D], BF16) for _ in range(NH)]
        b_H = [Buf() for _ in range(NH)]
        hT = [SB.alloc([128, KC, 512], BF16) for _ in range(2)]
        b_hT = [[Buf() for _ in range(4)] for _ in range(2)]
        st1 = [SB.alloc([128, 16], F32) for _ in range(2)]
        b_st1 = [Buf() for _ in range(2)]
        NQ = 3
        QK = [SB.alloc([128, 768], F32) for _ in range(NQ)]
        SQ = [SB.alloc([128, 768], F32) for _ in range(NQ)]
        QG = [SB.alloc([128, 768], F32) for _ in range(NQ)]
        QO = [SB.alloc([128, 768], BF16) for _ in range(NQ)]
        st2 = [SB.alloc([128, 48], F32) for _ in range(NQ)]
        b_QK, b_SQ, b_QG, b_QO, b_st2 = ([Buf() for _ in range(NQ)] for _ in range(5))
        NSTG = 2
        STG = [SB.alloc([128, 512], F32) for _ in range(NSTG)]
        b_STG = [Buf() for _ in range(NSTG)]
        STC = [SB.alloc([128, 512], F32) for _ in range(2)]
        b_STC = [Buf() for _ in range(2)]

        blocks = [(0, 256)] + [(CTX + 512 * j, 512) for j in range(8)]
        tinfo = []
        for bi, (t0, nb) in enumerate(blocks):
            for i in range(nb // 128):
                tinfo.append((bi, i, t0 // 128 + i, t0, nb, nb // 128))
        stg = {"i": 0}

        def stA(k):
            bi, i, g, t0, nb, ntile = tinfo[k]
            if k == 0:
                load_mix_reps(1)
            elif k == 2:
                load_mix_reps(0)
            xs = k % NX
            ts = k % 2
            tb = k % NTAB
            hb = k % NH
            R.dma("sp", XS[xs][:], seq_in[g * 128:(g + 1) * 128, :], writes=[b_XS[xs]])
            R.dma("sp", TAB[tb][:], d_cs[g * 128:(g + 1) * 128, :], writes=[b_TAB[tb]])
            R.op("act", ACT(TMP[ts][:], XS[xs][:], AF.Square, accum_out=st1[ts][:, 0:1]),
                 reads=[b_XS[xs]], writes=[b_TMP[ts], b_st1[ts]])
            R.op("dve", TS(st1[ts][:, 1:2], st1[ts][:, 0:1], 1.0 / D, EPS, ALU.mult, ALU.add),
                 reads=[b_st1[ts]], writes=[b_st1[ts]])
            R.op("pool", TT(st1[ts][:, 2:3], st1[ts][:, 1:2], neghalf[:, 0:1], ALU.pow),
                 reads=[b_st1[ts]], writes=[b_st1[ts]])
            R.op("dve", STT(TMP[ts][:], XS[xs][:], st1[ts][:, 2:3], GX[:], ALU.mult, ALU.mult),
                 reads=[b_XS[xs], b_st1[ts], b_rep], writes=[b_TMP[ts]])
            R.op("dve", TT(Hb[hb][:], TMP[ts][:], SX[:], ALU.add), reads=[b_TMP[ts], b_rep], writes=[b_H[hb]])

        def stP1(k):
            bi, i, g, t0, nb, ntile = tinfo[k]
            hs = bi % 2
            hb = k % NH
            qs = k % NQ
            tb = k % NTAB
            ps = k % 2
            tcols = slice(i * 128, (i + 1) * 128)
            fns = [TR(bank_bf(0)[:, kc * 128:(kc + 1) * 128], Hb[hb][:, kc * 128:(kc + 1) * 128], ident[:]) for kc in range(KC)]
            R.grp("pe", fns, reads=[b_H[hb], b_const], writes=[pbufs[0]])
            R.op("act", ACT(hT[hs][:, :, tcols], v3(bank_bf(0), 128), AF.Copy), reads=[pbufs[0]], writes=[b_hT[hs][i]])

        def stP2(k):
            bi, i, g, t0, nb, ntile = tinfo[k]
            hs = bi % 2
            qs = k % NQ
            tb = k % NTAB
            ps = k % 2
            tcols = slice(i * 128, (i + 1) * 128)
            pq = 1 + 2 * ps
            pkv = 2 + 2 * ps
            fns = [MM(bank(pq), hT[hs][:, kc, tcols], win[:, kc, 0:512], start=(kc == 0), stop=(kc == KC - 1)) for kc in range(KC)]
            fns += [MM(bank(pkv)[:, 0:384], hT[hs][:, kc, tcols], win[:, kc, 512:896], start=(kc == 0), stop=(kc == KC - 1))
                    for kc in range(KC)]
            R.grp("pe", fns, reads=[b_hT[hs][i], b_win], writes=[pbufs[pq], pbufs[pkv]])
            R.op("act", ACT(QK[qs][:, 0:512], bank(pq), AF.Copy), reads=[pbufs[pq]], writes=[b_QK[qs]])
            R.op("act", ACT(QK[qs][:, 512:768], bank(pkv)[:, 0:256], AF.Copy), reads=[pbufs[pkv]], writes=[b_QK[qs]])
            R.op("act", ACT(VB[:, g, :, 64:128], v3(bank(pkv)[:, 256:384], 64), AF.Copy), reads=[pbufs[pkv]], writes=[b_VB])
            R.op("dve", TT(QG[qs][:], QK[qs][:], gqk[:], ALU.mult), reads=[b_QK[qs], b_rep], writes=[b_QG[qs]])
            R.op("act", ACT(SQ[qs][:], QK[qs][:], AF.Square), reads=[b_QK[qs]], writes=[b_SQ[qs]])
            R.op("dve", RED(st2[qs][:, 0:12], v3(SQ[qs][:], 64), AX.X, ALU.add), reads=[b_SQ[qs]], writes=[b_st2[qs]])
            R.op("dve", TS(st2[qs][:, 12:24], st2[qs][:, 0:12], 1.0 / 64, EPS, ALU.mult, ALU.add),
                 reads=[b_st2[qs]], writes=[b_st2[qs]])
            R.op("pool", TT(st2[qs][:, 24:36], st2[qs][:, 12:24], neghalf[:, 0:12], ALU.pow),
                 reads=[b_st2[qs]], writes=[b_st2[qs]])
            R.op("pool", TT(v3(QK[qs][:], 64), v3(QG[qs][:], 64),
                            TAB[tb][:, 0:64].unsqueeze(1).to_broadcast([128, 12, 64]), ALU.mult),
                 reads=[b_QG[qs], b_TAB[tb], b_QK[qs]], writes=[b_QK[qs]])
            SQv = SQ[qs][:].rearrange("p (h t d) -> p h t d", t=2, d=32)
            QGv = QG[qs][:].rearrange("p (h t d) -> p h t d", t=2, d=32)
            R.op("dve", TT(SQv[:, :, 0, :], QGv[:, :, 1, :], TAB[tb][:, 64:96].unsqueeze(1).to_broadcast([128, 12, 32]), ALU.mult),
                 reads=[b_QG[qs], b_TAB[tb], b_SQ[qs], b_st2[qs]], writes=[b_SQ[qs]])
            R.op("dve", TT(SQv[:, :, 1, :], QGv[:, :, 0, :], TAB[tb][:, 96:128].unsqueeze(1).to_broadcast([128, 12, 32]), ALU.mult),
                 reads=[b_QG[qs], b_TAB[tb], b_SQ[qs], b_st2[qs]], writes=[b_SQ[qs]])
            R.op("pool", TT(QK[qs][:], QK[qs][:], SQ[qs][:], ALU.add), reads=[b_QK[qs], b_SQ[qs]], writes=[b_QK[qs]])
            R.op("pool", TT(v3(QO[qs][:], 64), v3(QK[qs][:], 64),
                            st2[qs][:, 24:36].unsqueeze(2).to_broadcast([128, 12, 64]), ALU.mult),
                 reads=[b_QK[qs], b_st2[qs]], writes=[b_QO[qs]])
            if i == ntile - 1:
                stFM(bi)

        def stFM(bi):
            t0, nb = blocks[bi]
            hs = bi % 2
            ntile = nb // 128
            for fc in range(10):
                pf = 6 + (fc % 2)
                fns = [MM(bank(pf)[:, 0:nb], win[:, kc, 896 + fc * 128:896 + (fc + 1) * 128], hT[hs][:, kc, 0:nb],
                          start=(kc == 0), stop=(kc == KC - 1)) for kc in range(KC)]
                R.grp("pe", fns, reads=b_hT[hs][0:ntile] + [b_win], writes=[pbufs[pf]])
                ch = fc % 2
                if fc < 6:
                    s_ = stg["i"] % NSTG
                    stg["i"] += 1
                    dst = (d_rT, d_gT, d_bT)[fc // 2]
                    R.op("act", ACT(STG[s_][:, 0:nb], bank(pf)[:, 0:nb], AF.Copy), reads=[pbufs[pf]], writes=[b_STG[s_]])
                    R.dma("act", dst[ch, :, t0:t0 + nb], STG[s_][:, 0:nb], reads=[b_STG[s_]])
                elif fc < 8:
                    R.op("act", ACT(STC[ch][:, 0:nb], bank(pf)[:, 0:nb], AF.Copy), reads=[pbufs[pf]], writes=[b_STC[ch]])
                else:
                    s_ = stg["i"] % NSTG
                    stg["i"] += 1
                    R.op("dve", TT(STG[s_][:, 0:nb], bank(pf)[:, 0:nb], STC[ch][:, 0:nb], ALU.mult),
                         reads=[pbufs[pf], b_STC[ch]], writes=[b_STG[s_]])
                    R.dma("pool", d_cuT[ch, :, t0:t0 + nb], STG[s_][:, 0:nb], reads=[b_STG[s_]])

        def stP3(k):
            bi, i, g, t0, nb, ntile = tinfo[k]
            qs = k % NQ
            gcols = slice(g * 128, (g + 1) * 128)
            fns = [TR(bank_bf(5)[:, j * 128:(j + 1) * 128], QO[qs][:, j * 128:(j + 1) * 128], ident[:]) for j in range(6)]
            R.grp("pe", fns, reads=[b_QO[qs], b_const], writes=[pbufs[5]])
            R.op("act", ACT(qT[:, :, gcols], v3(bank_bf(5)[:, 0:512], 128), AF.Copy), reads=[pbufs[5]], writes=[b_qT[g]])
            R.op("act", ACT(kT[0:64, 0::2, gcols], v3(bank_bf(5)[0:64, 512:768], 128), AF.Copy),
                 reads=[pbufs[5], b_VB], writes=[b_qT[g]])
            R.op("act", ACT(kT[64:128, 1::2, gcols], v3(bank_bf(5)[64:128, 512:768], 128), AF.Copy),
                 reads=[pbufs[5], b_VB], writes=[b_qT[g]])

        LA, L2, L3 = 3, 1, 2
        for k in range(NT + LA + L2 + L3):
            if k < NT:
                stA(k)
            if k % 2 == 1:
                bg_tick()
            if 0 <= k - LA < NT:
                stP1(k - LA)
            if 0 <= k - LA - L2 < NT:
                stP2(k - LA - L2)
            if 0 <= k - LA - L2 - L3 < NT:
                stP3(k - LA - L2 - L3)
        R.barrier()
        SB.release(p1mark)

        Pb = [SB.alloc([128, 1024], BF16) for _ in range(3)]
        b_P = [Buf() for _ in range(3)]
        REC = [SB.alloc([128, 512], F32) for _ in range(1)] * 2
        E0f = SB.alloc([128, 128], F32)
        E64f = SB.alloc([128, 128], F32)
        b_sel = Buf()
        R.op("pool", MS(E0f[:], 0.0), writes=[b_sel])
        R.op("pool", MS(E64f[:], 0.0), writes=[b_sel])
        R.op("pool", MS(E0f[0:1, :], 1.0), writes=[b_sel])
        R.op("pool", MS(E64f[64:65, :], 1.0), writes=[b_sel])
        BCS = [SB.alloc([128, 512], F32) for _ in range(2)]
        ATT = [SB.alloc([128, 512], BF16) for _ in range(2)]
        b_BCS, b_ATT = ([Buf() for _ in range(2)] for _ in range(2))
        b_REC = [Buf()] * 2
        R.op("pool", MS(REC[0][:], 0.0), writes=[b_REC[0]])
        lcw = SB.alloc([128, 8], F32)
        lcb = SB.alloc([128, 2], F32)
        lvec = SB.alloc([128, 12], F32)
        lder = SB.alloc([128, 48], F32)
        scw = SB.alloc([128, 6], F32)
        wbd = SB.alloc([128, 8, 128], BF16)
        RRG = SB.alloc([128, PADW], F32)
        RC = SB.alloc([128, PADW], F32)
        IG = SB.alloc([128, PADW], F32)
        T1 = SB.alloc([128, PADW], F32)
        HF = SB.alloc([128, PADW], F32)
        HBk = SB.alloc([128, PADW], F32)
        RCb = SB.alloc([128, PADW], BF16)
        NLB = (PADW + 511) // 512
        b_lc = Buf()
        b_RRG, b_RC, b_IG, b_T1, b_HF, b_HB, b_RCb = ([Buf() for _ in range(NLB)] for _ in range(7))
        lv3 = lvec[:].rearrange("p (i k) -> p i k", k=3)
        xcols = slice(4 + CTX, 4 + T)
        ccols = slice(2, 2 + CTX)
        lblocks = [(j * 512, min(512, PADW - j * 512)) for j in range(NLB)]
        lphases = []

        def l_setup():
            R.dma("sp", lcw[:], d_lcw[l].rearrange("p c k -> p (c k)"), writes=[b_lc])
            R.dma("sp", lcb[:], d_lcb[l], writes=[b_lc])
            R.dma("sp", lvec[:], d_lvec[l].rearrange("p a c k -> p (a c k)"), writes=[b_lc])
            R.dma("sp", scw[:], d_scw[l].rearrange("p c k -> p (c k)"), writes=[b_lc])
            for j in range(8):
                dd, gg, cc = j // 4, (j // 2) % 2, j % 2
                R.dma("pool", wbd[:, j, :], d_wbd[l, dd, gg, cc, :, :], writes=[b_lc])

        def l_pads():
            for tns, bb_ in ((HF, b_HF), (HBk, b_HB)):
                R.op("pool", MS(tns[:, 0:2], 0.0), writes=[bb_[0]])
                R.op("pool", MS(tns[:, 258:260], 0.0), writes=[bb_[0]])
                R.op("pool", MS(tns[:, 4356:PADW], 0.0), writes=[bb_[NLB - 1]])

        def l_setup2():
            R.op("act", ACT(lder[:, 16:20], lv3[:, :, 2], AF.Exp, scale=-1.0), reads=[b_lc], writes=[b_lc])
            R.op("act", ACT(lder[:, 20:24], lder[:, 16:20], AF.Ln, bias=1.0), reads=[b_lc], writes=[b_lc])

        def l_setup3():
            R.op("dve", TS(lder[:, 0:4], lv3[:, :, 0], -1.0, None, ALU.mult), reads=[b_lc], writes=[b_lc])
            R.op("dve", TS(lder[:, 4:8], lv3[:, :, 1], -1.0, None, ALU.mult), reads=[b_lc], writes=[b_lc])
            R.op("dve", TS(lder[:, 8:12], lder[:, 20:24], -8.0, None, ALU.mult), reads=[b_lc], writes=[b_lc])
            R.op("dve", TS(lder[:, 12:16], lder[:, 20:24], -16.0, None, ALU.mult), reads=[b_lc], writes=[b_lc])
        lphases.append((1, [[l_setup, l_setup2, l_setup3], [l_pads]]))

        def mk_load(dst, b_dst, src, c):
            def f():
                R.op("pool", MS(dst[:, 0:2], 0.0), writes=b_dst)
                R.op("pool", MS(dst[:, 258:260], 0.0), writes=b_dst)
                R.op("pool", MS(dst[:, 4356:PADW], 0.0), writes=b_dst)
                R.dma("sp", dst[:, ccols], src[c, :, 0:CTX], writes=b_dst)
                R.dma("sp", dst[:, xcols], src[c, :, CTX:T], writes=b_dst)
            return f

        def conv_block(j, c0, w, cw, nk, left, bias):
            a = max(c0, 2)
            b = min(c0 + w, 4358)
            n1 = b - a

            def f():
                for k in range(nk):
                    src = RRG[:, a + k - left:a + k - left + n1]
                    if k == 0:
                        if bias is not None:
                            R.op("dve", TS(RC[:, a:b], src, cw[:, 0:1], bias, ALU.mult, ALU.add), reads=b_RRG + [b_lc], writes=[b_RC[j]])
                        else:
                            R.op("dve", TS(RC[:, a:b], src, cw[:, 0:1], None, ALU.mult), reads=b_RRG + [b_lc], writes=[b_RC[j]])
                    else:
                        R.op("dve", STT(RC[:, a:b], src, cw[:, k:k + 1], RC[:, a:b], ALU.mult, ALU.add),
                             reads=b_RRG + [b_lc, b_RC[j]], writes=[b_RC[j]])
            return f

        for c in range(2):
            lphases.append((1, [[mk_load(RRG, b_RRG, d_rT, c)]]))
            lphases.append((1, [[lambda: None], [lambda: None], [lambda: None]]))
            steps = []
            for j, (c0, w) in enumerate(lblocks):
                cs_ = slice(c0, c0 + w)
                sub = [conv_block(j, c0, w, lcw[:, c * 4:c * 4 + 4], 4, 2, lcb[:, c:c + 1])]
                if j == 0:
                    sub.append(lambda: R.op("pool", MS(RC[:, 0:2], 0.0), writes=[b_RC[0]]))
                if j == NLB - 1:
                    sub.append(lambda: R.op("pool", MS(RC[:, 4358:PADW], 0.0), writes=[b_RC[NLB - 1]]))
                sub.append(lambda j=j, cs_=cs_: R.op("dve", CP(RCb[:, cs_], RC[:, cs_]), reads=[b_RC[j]], writes=[b_RCb[j]]))
                steps.append(sub)
            lphases.append((1, steps))
            for dd in range(2):
                idx = dd * 2 + c
                steps = []
                for j, (c0, w) in enumerate(lblocks):
                    cs_ = slice(c0, c0 + w)

                    def mma(j=j, cs_=cs_, w=w, dd=dd, c=c):
                        R.grp("pe", [MM(bank(7)[:, 0:w], wbd[:, dd * 4 + c, :], RCb[:, cs_])], reads=[b_RCb[j], b_lc], writes=[pbufs[7]])

                    def siga(j=j, cs_=cs_, w=w, idx=idx):
                        R.op("act", ACT(RRG[:, cs_], bank(7)[:, 0:w], AF.Exp, scale=-1.0, bias=lder[:, idx:idx + 1]),
                             reads=[pbufs[7], b_lc], writes=[b_RRG[j]])

                    def reca(j=j, cs_=cs_):
                        R.op("dve", TS(T1[:, cs_], RRG[:, cs_], 1.0, None, ALU.add), reads=[b_RRG[j]], writes=[b_T1[j]])
                        R.op("dve", RCP(RRG[:, cs_], T1[:, cs_]), reads=[b_T1[j]], writes=[b_RRG[j]])

                    def mmi(j=j, cs_=cs_, w=w, dd=dd, c=c):
                        R.grp("pe", [MM(bank(7)[:, 0:w], wbd[:, dd * 4 + 2 + c, :], RCb[:, cs_])], reads=[b_RCb[j], b_lc], writes=[pbufs[7]])

                    def sigi(j=j, cs_=cs_, w=w, idx=idx):
                        R.op("act", ACT(IG[:, cs_], bank(7)[:, 0:w], AF.Exp, scale=-1.0, bias=lder[:, 4 + idx:5 + idx]),
                             reads=[pbufs[7], b_lc], writes=[b_IG[j]])

                    def reci(j=j, cs_=cs_):
                        R.op("dve", TS(T1[:, cs_], IG[:, cs_], 1.0, None, ALU.add), reads=[b_IG[j]], writes=[b_T1[j]])
                        R.op("dve", RCP(IG[:, cs_], T1[:, cs_]), reads=[b_T1[j]], writes=[b_IG[j]])

                    def e4a(j=j, cs_=cs_, idx=idx):
                        R.op("act", ACT(RRG[:, cs_], RRG[:, cs_], AF.Exp, scale=lder[:, 8 + idx:9 + idx]), reads=[b_RRG[j], b_lc], writes=[b_RRG[j]])

                    def e4b(j=j, cs_=cs_):
                        R.op("pool", TT(T1[:, cs_], RRG[:, cs_], RRG[:, cs_], ALU.mult), reads=[b_RRG[j]], writes=[b_T1[j]])

                    def e4(j=j, cs_=cs_, idx=idx):
                        R.op("act", ACT(T1[:, cs_], T1[:, cs_], AF.Ln, scale=-1.0, bias=1.0), reads=[b_T1[j]], writes=[b_T1[j]])
                        R.op("act", ACT(T1[:, cs_], T1[:, cs_], AF.Exp, scale=0.5), reads=[b_T1[j]], writes=[b_T1[j]])

                    def d2(j=j, cs_=cs_):
                        R.op("pool", TT(IG[:, cs_], IG[:, cs_], RC[:, cs_], ALU.mult), reads=[b_IG[j], b_RC[j]], writes=[b_IG[j]])
                        R.op("pool", TT(T1[:, cs_], T1[:, cs_], IG[:, cs_], ALU.mult), reads=[b_T1[j], b_IG[j]], writes=[b_T1[j]])
                    steps.append([mma, siga, (lambda mmi=mmi, reca=reca: (mmi(), reca())), sigi, reci, (lambda: None), e4a, e4b, (lambda: None), e4, d2])
                lphases.append((0.25, steps))
                if dd == 0:
                    lphases.append((1, [[lambda: R.op("dve", SCAN(HF[:, ccols], RRG[:, ccols], T1[:, ccols], 0.0), reads=b_RRG + b_T1, writes=b_HF)],
                                        [lambda: R.op("dve", SCAN(HF[:, xcols], RRG[:, xcols], T1[:, xcols], HF[:, 257:258]),
                                                      reads=b_RRG + b_T1 + b_HF, writes=b_HF)]]))
                else:
                    lphases.append((1, [[lambda: R.op("dve", SCAN(HBk[:, 2:258][:, ::-1], RRG[:, 2:258][:, ::-1], T1[:, 2:258][:, ::-1], 0.0),
                                                      reads=b_RRG + b_T1, writes=b_HB)],
                                        [lambda: R.op("dve", SCAN(HBk[:, 260:4356][:, ::-1], RRG[:, 260:4356][:, ::-1], T1[:, 260:4356][:, ::-1],
                                                                  HBk[:, 2:3]), reads=b_RRG + b_T1 + b_HB, writes=b_HB)]]))
            lphases.append((1, [[mk_load(RRG, b_RRG, d_gT, c)]]))
            lphases.append((1, [[lambda: None], [lambda: None], [lambda: None]]))
            steps = []
            for j, (c0, w) in enumerate(lblocks):
                cs_ = slice(c0, c0 + w)

                def g1(j=j, cs_=cs_):
                    R.op("pool", TT(RC[:, cs_], RRG[:, cs_], RRG[:, cs_], ALU.mult), reads=[b_RRG[j]], writes=[b_RC[j]])
                    R.op("pool", TS(RC[:, cs_], RC[:, cs_], 0.044715, 1.0, ALU.mult, ALU.add), reads=[b_RC[j]], writes=[b_RC[j]])
                    R.op("pool", TT(RC[:, cs_], RC[:, cs_], RRG[:, cs_], ALU.mult), reads=[b_RC[j], b_RRG[j]], writes=[b_RC[j]])
                    R.op("pool", TT(HF[:, cs_], HF[:, cs_], HBk[:, cs_], ALU.add), reads=[b_HF[j], b_HB[j]], writes=[b_HF[j]])

                def g2(j=j, cs_=cs_):
                    R.op("act", ACT(IG[:, cs_], RC[:, cs_], AF.Exp, scale=-1.5957691216057308), reads=[b_RC[j]], writes=[b_IG[j]])

                def g3(j=j, cs_=cs_):
                    R.op("dve", TS(IG[:, cs_], IG[:, cs_], 1.0, None, ALU.add), reads=[b_IG[j]], writes=[b_IG[j]])
                    R.op("dve", RCP(T1[:, cs_], IG[:, cs_]), reads=[b_IG[j]], writes=[b_T1[j]])
                    R.op("dve", TT(T1[:, cs_], T1[:, cs_], RRG[:, cs_], ALU.mult), reads=[b_T1[j], b_RRG[j]], writes=[b_T1[j]])
                    R.op("dve", TT(RCb[:, cs_], T1[:, cs_], HF[:, cs_], ALU.mult), reads=[b_T1[j], b_HF[j]], writes=[b_RCb[j]])
                steps.append([g1, (lambda: None), (lambda: None), g2, g3])
            lphases.append((0.5, steps))

            def lout(c=c):
                R.dma("sp", d_mixT[4 + c, :, 0:CTX], RCb[:, ccols], reads=b_RCb)
                R.dma("sp", d_mixT[4 + c, :, CTX:T], RCb[:, xcols], reads=b_RCb)
            lphases.append((1, [[lout]]))
        for c in range(2):
            def loadb(c=c):
                R.dma("sp", HBk[:, ccols], d_bT[c, :, 0:CTX], writes=b_HB)
                R.dma("sp", HBk[:, xcols], d_bT[c, :, CTX:T], writes=b_HB)
            lphases.append((1, [[mk_load(RRG, b_RRG, d_cuT, c), loadb]]))
            lphases.append((1, [[lambda: None], [lambda: None], [lambda: None]]))
            steps = []
            for j, (c0, w) in enumerate(lblocks):
                a = max(c0, 2)
                b = min(c0 + w, 4356)

                def scm(j=j, a=a, b=b):
                    R.op("pool", TT(RCb[:, a:b], RC[:, a:b], HBk[:, a:b], ALU.mult), reads=[b_RC[j], b_HB[j]], writes=[b_RCb[j]])
                steps.append([conv_block(j, c0, w, scw[:, c * 3:c * 3 + 3], 3, 1, None), scm])
            lphases.append((1, steps))

            def sout(c=c):
                R.dma("sp", d_mixT[6 + c, :, 0:CTX], RCb[:, ccols], reads=b_RCb)
                R.dma("sp", d_mixT[6 + c, :, CTX:T], RCb[:, xcols], reads=b_RCb)
            lphases.append((1, [[sout]]))

        lstate = {"ph": 0, "next": 0, "active": []}

        def l_slot():
            act = lstate["active"]
            for stp in list(act):
                stp.pop(0)()
                if not stp:
                    act.remove(stp)
            while lstate["ph"] < len(lphases):
                rate, steps = lphases[lstate["ph"]]
                if lstate["next"] < len(steps):
                    lstate["tick"] = lstate.get("tick", 0) + 1
                    if rate >= 1:
                        nnew = int(rate)
                    else:
                        per = int(round(1 / rate))
                        nnew = 1 if lstate["tick"] % per == 1 else 0
                    for _ in range(nnew):
                        if lstate["next"] < len(steps):
                            act.append(list(steps[lstate["next"]]))
                            lstate["next"] += 1
                    return True
                if act:
                    return True
                lstate["ph"] += 1
                lstate["next"] = 0
            return bool(act)

        qblocks = []
        if not last:
            qblocks.append((0, 256, [0, 1]))
        for j in range(8):
            qblocks.append((CTX + 512 * j, 512, list(range(NT))))
        units = []
        ai = 0
        for (q0, nq, keys) in qblocks:
            for c in range(4):
                asl = ai % 2
                ai += 1
                for half in range(2):
                    h = 2 * c + half
                    u = dict(q0=q0, nq=nq, keys=keys, c=c, half=half, kv=h // 4, p0=half * 64, asl=asl,
                             npair=len(keys) // 2, idx=len(units))
                    u["o"] = 4 + (u["idx"] % 2)
                    u["us"] = u["idx"] % 2
                    if half == 0:
                        u.update(vsl=slice(64, 129), orows=slice(0, 65), dr=64, M=64)
                    else:
                        u.update(vsl=slice(0, 128), orows=slice(0, 128), dr=0, M=128)
                    u["hrows"] = slice(u["p0"], u["p0"] + 64)
                    u["kvar"] = u["kv"] * 2 + half
                    units.append(u)
        pairs = [(u, j) for u in units for j in range(u["npair"])]

        def emitS(i):
            u, j = pairs[i]
            sp_ = i % 2
            nq, q0, c = u["nq"], u["q0"], u["c"]
            fns = []
            for t_ in range(2):
                kt = u["keys"][2 * j + t_]
                fns.append(MM(pbig[sp_][:, t_ * 512:t_ * 512 + nq], kT[:, u["kvar"], kt * 128:(kt + 1) * 128], qT[:, c, q0:q0 + nq]))
            R.grp("pe", fns, reads=[], writes=[pbufs[2 * sp_], pbufs[2 * sp_ + 1]])

        def emitE(i):
            u, j = pairs[i]
            sp_ = i % 2
            pp_ = i % 3
            nq = u["nq"]
            if nq == 512:
                R.op("act", ACT(Pb[pp_][:, :], pbig[sp_][:, :], AF.Exp, scale=0.125),
                     reads=[pbufs[2 * sp_], pbufs[2 * sp_ + 1]], writes=[b_P[pp_]])
            else:
                R.op("act", ACT(v3(Pb[pp_][:, :], 512)[:, :, 0:nq], v3(pbig[sp_][:, :], 512)[:, :, 0:nq], AF.Exp, scale=0.125),
                     reads=[pbufs[2 * sp_], pbufs[2 * sp_ + 1]], writes=[b_P[pp_]])

        def emitPV(i):
            u, j = pairs[i]
            pp_ = i % 3
            nq, o = u["nq"], u["o"]
            fns = []
            for t_ in range(2):
                kt = u["keys"][2 * j + t_]
                fns.append(MM(bank(o)[u["orows"], 0:nq], VB[:, kt, u["kv"], u["vsl"]], Pb[pp_][:, t_ * 512:t_ * 512 + nq],
                              start=(j == 0 and t_ == 0), stop=(j == u["npair"] - 1 and t_ == 1)))
            R.grp("pe", fns, reads=[b_P[pp_]], writes=[pbufs[o]])

        def mk_norm(u):
            us, o, dr, M, hrows, asl, nq, half, c, q0 = (u[k_] for k_ in ("us", "o", "dr", "M", "hrows", "asl", "nq", "half", "c", "q0"))

            def rcp():
                R.op("dve", RCP(REC[us][dr:dr + 1, 0:nq], bank(o)[dr:dr + 1, 0:nq]), reads=[pbufs[o]], writes=[b_REC[us]])

            def norm():
                R.grp("pe", [MM(bank(6)[0:M, 0:nq], (E64f if dr == 64 else E0f)[:, 0:M], REC[us][:, 0:nq])],
                      reads=[b_REC[us], b_sel], writes=[pbufs[6]])
                R.op("dve", CP(BCS[us][hrows, 0:nq], bank(6)[hrows, 0:nq]), reads=[pbufs[6]], writes=[b_BCS[us]])
                R.op("dve", TT(ATT[asl][hrows, 0:nq], bank(o)[hrows, 0:nq], BCS[us][hrows, 0:nq], ALU.mult),
                     reads=[pbufs[o], b_BCS[us]], writes=[b_ATT[asl]])
                if half == 1:
                    R.dma("sp", d_mixT[c, :, q0:q0 + nq], ATT[asl][:, 0:nq], reads=[b_ATT[asl]])
            norm.rcp = rcp
            return norm

        pending = []
        npairs = len(pairs)
        emitS(0)
        if npairs > 1:
            emitS(1)
        for i, (u, j) in enumerate(pairs):
            npair = u["npair"]
            emitE(i)
            if i + 2 < npairs:
                emitS(i + 2)
            emitPV(i)
            if j == min(8, npair - 1) and pending:
                pending.pop()()
            if j in (1, 4, 7, 10, 13) or (npair < 6 and j == npair - 1):
                l_slot()
            if j == npair - 1:
                nf_ = mk_norm(u)
                nf_.rcp()
                pending.append(nf_)
                if u["idx"] % 3 == 0:
                    bg_tick()
        while pending:
            pending.pop()()
        if last:
            while bgq:
                bg_tick()
        else:
            while bgq:
                bg_tick()
        nflush = 0
        while l_slot():
            nflush += 1
        R.barrier()
        SB.release(lmark)

        WD = SB.alloc([128, NF, D], BF16)
        WO = SB.alloc([128, KC, D], BF16)
        b_w3 = Buf()
        R.dma("sp", WO[:, :, :], d_woutb[l].rearrange("(k p) n -> p k n", p=128), reads=[b_w3b[l]], writes=[b_w3])
        b_wd = Buf()
        wd_loads = []
        for f0 in range(0, NF, 4):
            f1 = min(NF, f0 + 4)
            wd_loads.append(lambda f0=f0, f1=f1: R.dma("sp", WD[:, f0:f1, :], d_wdnb[l, f0 * 128:f1 * 128, :].rearrange("(f p) n -> p f n", p=128),
                                                     reads=[b_w3b[l]], writes=[b_wd]))
        fcw = SB.alloc([128, 132], F32)
        R.dma("sp", fcw[:], d_fcw[l].rearrange("p c k -> p (c k)"), writes=[b_w3])
        GA, GFm, SFm, GFg = (SB.alloc([128, D], F32) for _ in range(4))
        b_rep3 = Buf()
        GFIN = None
        if last:
            GFIN = SB.alloc([128, D], F32)
            R.dma("sp", GFIN[:], d_gfin[0:1, :].partition_broadcast(128), writes=[b_w3])
        TMP3 = [SB.alloc([128, D], F32) for _ in range(2)]
        b_TMP3 = [Buf() for _ in range(2)]

        def load_reps_b(row):
            load_rep(GA, l, row, 2, b_rep3)
            load_rep(SFm, l, row, 3, b_rep3)
            load_rep(GFm, l, row, 4, b_rep3)
            R.dma("sp", TMP3[0][:], d_gffn[l:l + 1, :].partition_broadcast(128), writes=[b_TMP3[0]])
            R.op("dve", TT(GFm[:], GFm[:], TMP3[0][:], ALU.mult), reads=[b_rep3, b_TMP3[0]], writes=[b_rep3])

        def load_reps_g(row):
            load_rep(GFg, l, row, 5, b_repg)

        b_repg = Buf()
        NWU = 4
        WU = [SB.alloc([128, 2048], BF16) for _ in range(NWU)]
        b_WU = [Buf() for _ in range(NWU)]
        NXW = 8
        XW = [SB.alloc([128, D], F32) for _ in range(NXW)]
        b_XW = [Buf() for _ in range(NXW)]
        for j in range(NXW):
            R.op("dve", MS(XW[j][:], 0.0), writes=[b_XW[j]])
        H2 = [SB.alloc([128, D], BF16) for _ in range(2)]
        b_H2 = [Buf() for _ in range(2)]
        st3 = [SB.alloc([128, 16], F32) for _ in range(2)]
        b_st3 = [Buf() for _ in range(2)]
        H2T = [SB.alloc([128, KC, 512], BF16) for _ in range(2)]
        b_H2T = [[Buf() for _ in range(4)] for _ in range(2)]
        MW = [SB.alloc([128, KC, 512], BF16) for _ in range(2)]
        b_MW = [Buf() for _ in range(2)]
        for j in range(2):
            R.op("dve", MS(MW[j][:], 0.0), writes=[b_MW[j]])
        AT = SB.alloc([128, NF, 512], BF16)
        b_AT = Buf()
        R.op("pool", MS(AT[:], 0.0), writes=[b_AT])
        NG = 2
        CUb = [SB.alloc([128, 512], F32) for _ in range(NG)]
        G0 = [SB.alloc([128, 512], F32) for _ in range(NG)]
        G1 = [SB.alloc([128, 512], F32) for _ in range(NG)]
        G2 = [SB.alloc([128, 512], F32) for _ in range(NG)]
        b_CU, b_G0, b_G1, b_G2 = ([Buf() for _ in range(NG)] for _ in range(4))

        windows = []
        if not last:
            windows.append((0, CTX, 0, CTX, 1))
        s_ = CTX
        while s_ < T:
            n_ = min(510, T - s_)
            windows.append((CTX, T, s_, n_, 0))
            s_ += n_
        st = {"row": None, "xw": 0, "pj": 0, "tr": 0, "up": 0, "wu": 0, "tk": 0, "g": 0}
        winfo = {}

        def proj_half(fn_mm, m):
            pj = st["pj"] % 2
            st["pj"] += 1
            return pj

        def stageBL_list(wi):
            seg_lo, seg_hi, s, n, mrow = windows[wi]
            W = n + 2
            wlo = s - 1
            tiles = [(c0, min(128, W - c0)) for c0 in range(0, W, 128)]
            ms = wi % 2
            hsl = wi % 2
            v0 = max(wlo, seg_lo)
            v1 = min(wlo + W, seg_hi)
            out = []
            for k0 in (0, 4):
                out.append(lambda k0=k0: R.dma("sp", MW[ms][:, k0:k0 + 4, v0 - wlo:v1 - wlo],
                                               d_mixT[k0:k0 + 4, :, v0:v1].rearrange("k p t -> p k t"), writes=[b_MW[ms]]))
            xslots = []
            for ti_, (c0, m) in enumerate(tiles):
                xs = st["xw"] % NXW
                st["xw"] += 1
                xslots.append(xs)
                r0 = max(wlo + c0, seg_lo)
                r1 = min(wlo + c0 + m, seg_hi)
                p0_ = r0 - (wlo + c0)
                if r1 > r0:
                    out.append(lambda xs=xs, p0_=p0_, r0=r0, r1=r1: R.dma("sp", XW[xs][p0_:p0_ + (r1 - r0), :], seq_in[r0:r1, :],
                                                                         writes=[b_XW[xs]]))
            winfo[wi] = (W, wlo, tiles, xslots, hsl)
            return out

        def stageBL(wi):
            for f_ in stageBL_list(wi):
                f_()

        def stageB_steps(wi):
            seg_lo, seg_hi, s, n, mrow = windows[wi]
            W, wlo, tiles, xslots, hsl = winfo[wi]
            ms = wi % 2
            ops, trs = [], []
            tsl = {}

            def mk_op(ti_, c0, m):
                def op():
                    if ti_ == 0 and st["row"] != mrow:
                        load_reps_b(mrow)
                        st["row"] = mrow
                    xs = xslots[ti_]
                    ts = st["tk"] % 2
                    st["tk"] += 1
                    tsl[ti_] = ts
                    for hf in range(2):
                        pj = st["pj"] % 2
                        st["pj"] += 1
                        hc = slice(hf * 512, (hf + 1) * 512)
                        fns = [MM(bank(pj)[0:m, :], MW[ms][:, kc, c0:c0 + m], WO[:, kc, hc], start=(kc == 0), stop=(kc == KC - 1))
                               for kc in range(KC)]
                        R.grp("pe", fns, reads=[b_MW[ms], b_w3], writes=[pbufs[pj]])
                        R.op("dve", TT(TMP3[ts][0:m, hc], bank(pj)[0:m, :], GA[0:m, hc], ALU.mult),
                             reads=[pbufs[pj], b_rep3], writes=[b_TMP3[ts]])
                    R.op("dve", TT(XW[xs][0:m, :], XW[xs][0:m, :], TMP3[ts][0:m, :], ALU.add), reads=[b_TMP3[ts], b_XW[xs]], writes=[b_XW[xs]])
                    R.op("act", ACT(TMP3[ts][:], XW[xs][:], AF.Square, accum_out=st3[ts][:, 0:1]),
                         reads=[b_XW[xs]], writes=[b_TMP3[ts], b_st3[ts]])
                    R.op("dve", TS(st3[ts][:, 1:2], st3[ts][:, 0:1], 1.0 / D, EPS, ALU.mult, ALU.add), reads=[b_st3[ts]], writes=[b_st3[ts]])
                    R.op("pool", TT(st3[ts][:, 2:3], st3[ts][:, 1:2], neghalf[:, 0:1], ALU.pow), reads=[b_st3[ts]], writes=[b_st3[ts]])
                    R.op("dve", STT(TMP3[ts][:], XW[xs][:], st3[ts][:, 2:3], GFm[:], ALU.mult, ALU.mult),
                         reads=[b_XW[xs], b_st3[ts], b_rep3], writes=[b_TMP3[ts]])
                    R.op("pool", TT(H2[ts][:], TMP3[ts][:], SFm[:], ALU.add), reads=[b_TMP3[ts], b_rep3], writes=[b_H2[ts]])
                return op

            def mk_tr(ti_, c0, m):
                def tr():
                    ts = tsl[ti_]
                    tb = (2, 7)[st["tr"] % 2]
                    st["tr"] += 1
                    fns = [TR(bank_bf(tb)[:, kc * 128:(kc + 1) * 128], H2[ts][:, kc * 128:(kc + 1) * 128], ident[:]) for kc in range(KC)]
                    R.grp("pe", fns, reads=[b_H2[ts], b_const], writes=[pbufs[tb]])
                    R.op("act", ACT(H2T[hsl][:, :, c0:c0 + m], v3(bank_bf(tb), 128)[:, :, 0:m], AF.Copy), reads=[pbufs[tb]], writes=[b_H2T[hsl][ti_]])
                    if ti_ == 0 and wlo < seg_lo:
                        R.op("pool", MS(H2T[hsl][:, :, 0:1], 0.0), reads=[], writes=[b_H2T[hsl][0]])
                    if ti_ == len(tiles) - 1 and wlo + W > seg_hi:
                        R.op("pool", MS(H2T[hsl][:, :, W - 1:W], 0.0), reads=[], writes=[b_H2T[hsl][len(tiles) - 1]])
                return tr

            for ti_, (c0, m) in enumerate(tiles):
                ops.append(mk_op(ti_, c0, m))
                trs.append(mk_tr(ti_, c0, m))
            return ops, trs

        def run_B_interleaved(wi, fillers):
            ops, trs = stageB_steps(wi)
            nt = len(ops)
            fill = list(fillers)
            for i in range(nt):
                ops[i]()
                if i >= 1:
                    trs[i - 1]()
            if fill:
                fill.pop(0)()
            trs[nt - 1]()
            for f_ in fill:
                f_()

        def stageUp(wi, mid=None):
            W, wlo, tiles, xslots, hsl = winfo[wi]
            upend = []
            nv = W - 2
            hdeps = b_H2T[hsl][0:len(tiles)]
            midl = mid() if mid is not None else []
            for f in range(NF):
                if f % 2 == 1 and wd_loads:
                    wd_loads.pop(0)()
                if f >= 4 and f % 2 == 0 and midl:
                    midl.pop(0)()
                ws_ = st["wu"] % NWU
                st["wu"] += 1
                R.dma("sp", WU[ws_][:], d_wupb[l, f * 128:(f + 1) * 128, :], reads=[b_wupb], writes=[b_WU[ws_]])
                bu = 3 + 2 * (st["up"] % 2)
                bg = bu + 1
                st["up"] += 1
                gs = st["g"] % NG
                st["g"] += 1
                fns = [MM(bank(bu)[:, 0:W], WU[ws_][:, kc * 256:kc * 256 + 128], H2T[hsl][:, kc, 0:W], start=(kc == 0), stop=(kc == KC - 1))
                       for kc in range(KC)]
                R.grp("pe", fns, reads=[b_WU[ws_]] + hdeps, writes=[pbufs[bu]])
                fns = [MM(bank(bg)[:, 0:W], WU[ws_][:, kc * 256 + 128:kc * 256 + 256], H2T[hsl][:, kc, 0:W], start=(kc == 0), stop=(kc == KC - 1))
                       for kc in range(KC)]
                R.grp("pe", fns, reads=[b_WU[ws_]] + hdeps, writes=[pbufs[bg]])
                cu_ = f * 3
                cg_ = (NF + f) * 3
                R.op("dve", TS(CUb[gs][:, 0:nv], bank(bu)[:, 0:nv], fcw[:, cu_:cu_ + 1], None, ALU.mult),
                     reads=[pbufs[bu], b_w3], writes=[b_CU[gs]])
                for k in (1, 2):
                    R.op("dve", STT(CUb[gs][:, 0:nv], bank(bu)[:, k:k + nv], fcw[:, cu_ + k:cu_ + k + 1], CUb[gs][:, 0:nv], ALU.mult, ALU.add),
                         reads=[pbufs[bu], b_w3, b_CU[gs]], writes=[b_CU[gs]])
                Gk = (G0, G1, G2)
                bGk = (b_G0, b_G1, b_G2)
                for k in range(3):
                    R.op("act", ACT(Gk[k][gs][:, 0:nv], bank(bg)[:, k:k + nv], AF.Copy, scale=fcw[:, cg_ + k:cg_ + k + 1]),
                         reads=[pbufs[bg], b_w3], writes=[bGk[k][gs]])
                R.op("pool", TT(G0[gs][:, 0:nv], G0[gs][:, 0:nv], G1[gs][:, 0:nv], ALU.add),
                     reads=[b_G0[gs], b_G1[gs]], writes=[b_G0[gs]])
                R.op("pool", TT(G0[gs][:, 0:nv], G0[gs][:, 0:nv], G2[gs][:, 0:nv], ALU.add),
                     reads=[b_G0[gs], b_G2[gs]], writes=[b_G0[gs]])

                def fin(gs=gs, f=f, nv=nv):
                    R.op("act", ACT(G1[gs][:, 0:nv], G0[gs][:, 0:nv], AF.Silu), reads=[b_G0[gs]], writes=[b_G1[gs]])
                    R.op("dve", TT(AT[:, f, 1:1 + nv], G1[gs][:, 0:nv], CUb[gs][:, 0:nv], ALU.mult),
                         reads=[b_G1[gs], b_CU[gs]], writes=[b_AT])
                if upend:
                    upend.pop()()
                upend.append(fin)
            while upend:
                upend.pop()()
            while midl:
                midl.pop(0)()

        def down_steps(wi):
            W, wlo, tiles, xslots, hsl = winfo[wi]
            mrow_d = windows[wi][4]
            steps = []

            def mk(ti_, c0, m):
                def dn():
                    if ti_ == 0 and st.get("rowg") != mrow_d:
                        load_reps_g(mrow_d)
                        st["rowg"] = mrow_d
                    xs = xslots[ti_]
                    ts = st["tk"] % 2
                    st["tk"] += 1
                    for hf in range(2):
                        pj = st["pj"] % 2
                        st["pj"] += 1
                        hc = slice(hf * 512, (hf + 1) * 512)
                        fns = [MM(bank(pj)[0:m, :], AT[:, f, c0:c0 + m], WD[:, f, hc], start=(f == 0), stop=(f == NF - 1)) for f in range(NF)]
                        R.grp("pe", fns, reads=[b_AT, b_w3, b_wd], writes=[pbufs[pj]])
                        R.op("dve", TT(TMP3[ts][0:m, hc], bank(pj)[0:m, :], GFg[0:m, hc], ALU.mult),
                             reads=[pbufs[pj], b_repg], writes=[b_TMP3[ts]])
                    R.op("dve", TT(XW[xs][0:m, :], XW[xs][0:m, :], TMP3[ts][0:m, :], ALU.add), reads=[b_TMP3[ts], b_XW[xs]], writes=[b_XW[xs]])
                    a0 = max(c0, 1)
                    a1 = min(c0 + m, W - 1)
                    if a1 <= a0:
                        return
                    pa0 = a0 - c0
                    npart = a1 - a0
                    row0 = wlo + a0
                    if not last:
                        R.dma("pool", d_res[row0:row0 + npart, :], XW[xs][pa0:pa0 + npart, :], reads=[b_XW[xs]])
                    else:
                        R.op("act", ACT(TMP3[ts][:], XW[xs][:], AF.Square, accum_out=st3[ts][:, 0:1]),
                             reads=[b_XW[xs]], writes=[b_TMP3[ts], b_st3[ts]])
                        R.op("dve", TS(st3[ts][:, 1:2], st3[ts][:, 0:1], 1.0 / D, EPS, ALU.mult, ALU.add), reads=[b_st3[ts]], writes=[b_st3[ts]])
                        R.op("pool", TT(st3[ts][:, 2:3], st3[ts][:, 1:2], neghalf[:, 0:1], ALU.pow), reads=[b_st3[ts]], writes=[b_st3[ts]])
                        R.op("dve", STT(XW[xs][:], XW[xs][:], st3[ts][:, 2:3], GFIN[:], ALU.mult, ALU.mult),
                             reads=[b_XW[xs], b_st3[ts], b_w3], writes=[b_XW[xs]])
                        R.dma("pool", d_y[row0 - CTX:row0 - CTX + npart, :], XW[xs][pa0:pa0 + npart, :], reads=[b_XW[xs]])
                return dn
            for ti_, (c0, m) in enumerate(tiles):
                steps.append(mk(ti_, c0, m))
            return steps

        nwin = len(windows)
        wi = 0
        stageBL(wi)
        run_B_interleaved(wi, [])
        while wi < nwin:
            if wi + 1 < nwin:
                stageUp(wi, mid=lambda w=wi + 1: stageBL_list(w))
            else:
                stageUp(wi)
            dsteps = down_steps(wi)
            if wi + 1 < nwin:
                run_B_interleaved(wi + 1, dsteps)
            else:
                for d_ in dsteps:
                    d_()
            wi += 1
        R.barrier()
        SB.release(lmark)

    for l in range(DEPTH):
        layer(l)
    R.barrier()
    R.emit(nc)
    return nc, SB.peak


def _host_prep(inp):
    f = np.float32
    g = {k: np.asarray(v) for k, v in inp.items()}
    L = DEPTH
    w_in = g["w_in"]
    q = w_in[:, :, 0:512]
    k0 = w_in[:, :, 512:576]
    k1 = w_in[:, :, 576:640]
    v = w_in[:, :, 640:768]
    rest = w_in[:, :, 768:2048]
    w_in_e = np.ascontiguousarray(np.concatenate([q, k0, k0, k1, k1, v, rest], axis=2), dtype=f)
    gqk = np.ascontiguousarray(np.concatenate([np.tile(g["g_q"], (1, 8)), np.tile(g["g_k"], (1, 4))], axis=1), dtype=f)
    rows = SEQ // 64
    pos_r = np.repeat(np.arange(rows, dtype=f), 64)
    pos_c = np.tile(np.arange(64, dtype=f), rows)
    inv = (f(10000.0) ** (-np.arange(16, dtype=f) / f(16))).astype(f)
    ang = np.concatenate([pos_r[:, None] * inv, pos_c[:, None] * inv], axis=-1).astype(f)
    cos = np.cos(ang).astype(f)
    sin = np.sin(ang).astype(f)
    cs = np.zeros((T, 128), f)
    cs[:CTX, 0:64] = 1.0
    cs[CTX:, 0:32] = cos
    cs[CTX:, 32:64] = cos
    cs[CTX:, 64:96] = -sin
    cs[CTX:, 96:128] = sin
    lru_cw = np.ascontiguousarray(g["lru_conv_w"].reshape(L, 4, 2, 128).transpose(0, 3, 2, 1), dtype=f)
    lru_cb = np.ascontiguousarray(g["lru_conv_b"].reshape(L, 2, 128).transpose(0, 2, 1), dtype=f)
    wbd = np.zeros((L, 2, 2, 2, 128, 128), f)
    for gi, nm in enumerate(("lru_wa", "lru_wi")):
        w = g[nm]
        for c in range(2):
            for j in range(2):
                wbd[:, :, gi, c, j * 64:(j + 1) * 64, j * 64:(j + 1) * 64] = w[:, :, 2 * c + j]
    lvec = np.stack([g["lru_ba"], g["lru_bi"], g["lru_lam"]], axis=-1)
    lvec = np.ascontiguousarray(lvec.reshape(L, 2, 2, 128, 3).transpose(0, 3, 1, 2, 4), dtype=f)
    sc_cw = np.ascontiguousarray(g["sc_conv_w"].reshape(L, 3, 2, 128).transpose(0, 3, 2, 1), dtype=f)
    wu = g["w_up"].reshape(L, KC, 128, 2, NF, 128)
    wup_r = np.ascontiguousarray(wu.transpose(0, 4, 2, 1, 3, 5).reshape(L, NF * 128, 2048), dtype=f)
    fcw = np.ascontiguousarray(g["ffn_conv_w"].reshape(L, 3, 44, 128).transpose(0, 3, 2, 1), dtype=f)
    shared = {
        "w_mod": np.ascontiguousarray(g["w_mod"], dtype=f), "b_mod": np.ascontiguousarray(g["b_mod"], dtype=f),
        "g_mix": np.ascontiguousarray(g["g_mix"], dtype=f), "g_ffn": np.ascontiguousarray(g["g_ffn"], dtype=f),
        "g_final": np.ascontiguousarray(g["g_final"].reshape(1, D), dtype=f),
        "w_in_e": w_in_e, "gqk": gqk, "cs": cs, "lru_cw": lru_cw, "lru_cb": lru_cb, "wbd": wbd, "lru_vec": lvec,
        "sc_cw": sc_cw, "w_out": np.ascontiguousarray(g["w_out"], dtype=f), "wup_r": wup_r, "fcw": fcw,
        "w_down": np.ascontiguousarray(g["w_down"], dtype=f), "ident": np.eye(128, dtype=f),
    }
    maps = []
    for b in range(8):
        m = dict(shared)
        m["xin"] = np.ascontiguousarray(np.concatenate([g["ctx"][b], g["x"][b]], axis=0), dtype=f)
        cv = np.stack([g["c"][b], g["c_ctx"]], axis=0)
        m["cT"] = np.ascontiguousarray(cv.reshape(2, KC, 128).transpose(2, 1, 0).reshape(128, 16), dtype=f)
        maps.append(m)
    return maps


_NC_CACHE = {}


def kernel(**inputs):
    maps = _host_prep(inputs)
    if "nc" not in _NC_CACHE:
        _NC_CACHE["nc"] = build_nc()[0]
    nc = _NC_CACHE["nc"]
    res = run_bass_kernel_spmd(nc, maps, core_ids=list(range(8)))
    out = np.stack([np.asarray(r["y"], dtype=np.float32) for r in res.results], axis=0)
    return out
```
